# Optimizing a Trainium2 kernel written in Bass

```python
import math
import jax
import jax.numpy as jnp
from jax import lax
import numpy as np

D_MODEL = 2048
BATCH = 16
SEQ = 256
DEPTH = 2
DEC_BATCH = 8
DEC_SEQ = 2048
PAST_LEN = 256

GRID_W = 64
CONV_W = 3
N_DIR = 2
MIX_WIDTH = D_MODEL
GROUP_WIDTH = MIX_WIDTH // 4
HY_CH = GROUP_WIDTH
HY_ORDER = 2
HY_BANDS = 8
HY_EMB = 1 + 2 * HY_BANDS
HY_FFN = 64
HY_SINS = 2
HY_DECAY_TARGET = 1e-2
HY_FAST_PCT = 0.3
HY_SLOW_PCT = 1.5
SSD_WIDTH = GROUP_WIDTH
SSD_HEAD_DIM = 64
SSD_HEADS = SSD_WIDTH // SSD_HEAD_DIM
SSD_GROUPS = 2
SSD_HPG = SSD_HEADS // SSD_GROUPS
SSD_STATE = 128
SSD_CHUNK = 128
GDN_WIDTH = GROUP_WIDTH
GDN_HEAD_DIM = 128
GDN_HEADS = GDN_WIDTH // GDN_HEAD_DIM
GDN_CHUNK = 64
RET_WIDTH = GROUP_WIDTH
RET_HEAD_DIM = 128
RET_HEADS = RET_WIDTH // RET_HEAD_DIM
RET_CHUNK = 128
ROPE_BASE = 10000.0
D_FF = 5632
CONV_SIZES = (HY_CH, HY_CH, HY_CH, SSD_WIDTH, SSD_GROUPS * SSD_STATE, SSD_GROUPS * SSD_STATE, GDN_WIDTH, GDN_WIDTH, GDN_WIDTH)
REST_SIZES = (SSD_WIDTH, N_DIR * SSD_HEADS, GDN_WIDTH, N_DIR * GDN_HEADS, N_DIR * GDN_HEADS, RET_WIDTH, RET_WIDTH, RET_WIDTH, RET_WIDTH)
N_CONV = sum(CONV_SIZES)
N_IN = N_CONV + sum(REST_SIZES)
DN_ALPHA = (2 * DEPTH) ** 0.25
DN_BETA = (8 * DEPTH) ** -0.25
LN_EPS = 1e-5
RMS_EPS = 1e-6

kernel_name = 'hybrid_diffusion_hyena_ssd_gdn_retention_step'


def norm_plain(x):
    mu = jnp.mean(x, -1, keepdims=True)
    var = jnp.mean(jnp.square(x - mu), -1, keepdims=True)
    return (x - mu) * lax.rsqrt(var + LN_EPS)


def layer_norm(x, g, b):
    return norm_plain(x.astype(jnp.float32)).astype(x.dtype) * g + b


def rms_norm(x, g):
    xf = x.astype(jnp.float32)
    return (xf * lax.rsqrt(jnp.mean(xf * xf, -1, keepdims=True) + RMS_EPS)).astype(x.dtype) * g


def l2norm(x):
    return x * lax.rsqrt(jnp.sum(x * x, -1, keepdims=True) + 1e-6)


def split_sizes(x, sizes):
    return jnp.split(x, np.cumsum(sizes)[:-1].tolist(), axis=-1)


def dwconv(x, w, b):
    y = lax.conv_general_dilated(x, w[:, None, :].astype(x.dtype), window_strides=(1,),
                                 padding=((CONV_W // 2, CONV_W // 2),),
                                 dimension_numbers=('NWC', 'WIO', 'NWC'),
                                 feature_group_count=x.shape[-1])
    return y + b.astype(x.dtype)


def run_direction(fn, d, seqs, *args):
    if d == 1:
        seqs = tuple(jnp.flip(t, 1) for t in seqs)
    y, s = fn(*seqs, *args)
    if d == 1:
        y = jnp.flip(y, 1)
    return y, s


def to_chunks(t, q):
    b, L, h = t.shape[:3]
    return jnp.moveaxis(t.reshape(b, L // q, q, h, *t.shape[3:]), 3, 2)


def from_chunks(o):
    b, nc, h, q, v = o.shape
    return jnp.moveaxis(o, 2, 3).reshape(b, nc * q, h, v)


def chunk_state_scan(s0, q_dec, k_dec, val, attn, tot, w=None):
    arrs = (q_dec, k_dec, val, attn, tot) if w is None else (q_dec, k_dec, val, attn, tot, w)
    xs = tuple(jnp.moveaxis(t, 1, 0) for t in arrs)

    def step(s, xc):
        qd, kd, vv, at, tt = xc[:5]
        if w is not None:
            vv = vv - jnp.einsum('bhqk,bhkv->bhqv', xc[5], s)
        o = jnp.einsum('bhqk,bhkv->bhqv', qd, s) + jnp.einsum('bhij,bhjv->bhiv', at, vv)
        s = s * tt[..., None, None] + jnp.einsum('bhqk,bhqv->bhkv', kd, vv)
        return s, o

    s_fin, o = lax.scan(step, s0, xs)
    return jnp.moveaxis(o, 0, 1), s_fin


def hyena_order2(v, x1, x2, w1, b1, w2, b2, w3, freq, bias):
    b, L, C = v.shape
    f32 = jnp.float32
    pos = jnp.arange(L, dtype=f32)
    t = pos / (L - 1)
    bands = jnp.linspace(1e-4, HY_BANDS - 1, HY_BANDS, dtype=f32)
    ang = (2.0 * math.pi / L) * pos[:, None] * bands
    feats = jnp.concatenate([t[:, None], jnp.cos(ang), -jnp.sin(ang)], -1)
    h = jnp.sin(freq[0].astype(f32) * (feats @ w1.astype(f32) + b1.astype(f32)))
    h = jnp.sin(freq[1].astype(f32) * (h @ w2.astype(f32) + b2.astype(f32)))
    filt = (h @ w3.astype(f32)).reshape(L, HY_ORDER, N_DIR, C)
    deltas = jnp.abs(jnp.linspace(math.log(HY_DECAY_TARGET) / HY_SLOW_PCT,
                                  math.log(HY_DECAY_TARGET) / HY_FAST_PCT, C, dtype=f32))
    filt = filt * jnp.exp(-t[:, None] * deltas)[:, None, None, :]
    kern = jnp.concatenate([filt[:, :, 0], jnp.zeros((1, HY_ORDER, C), f32), filt[:0:-1, :, 1]], 0)
    kern = kern / jnp.sum(jnp.abs(kern), 0, keepdims=True)
    kf = jnp.fft.rfft(kern, axis=0)

    def long_conv(z, o):
        zf = jnp.fft.rfft(z.astype(f32), n=2 * L, axis=1)
        y = jnp.fft.irfft(zf * kf[:, o], n=2 * L, axis=1)[:, :L]
        return (y + z.astype(f32) * bias[o].astype(f32)).astype(v.dtype)

    z = x1 * long_conv(v, 0)
    return x2 * long_conv(z, 1)


def ssd_chunked(x, dt, a, bm, cm, h0):
    b, L, G, E, P = x.shape
    N = bm.shape[-1]
    Q = SSD_CHUNK
    nc = L // Q
    xs = (x * dt[..., None]).reshape(b, nc, Q, G, E, P)
    bc = bm.reshape(b, nc, Q, G, N)
    cc = cm.reshape(b, nc, Q, G, N)
    acs = jnp.cumsum(a.reshape(b, nc, Q, G, E), axis=2)
    causal = jnp.tril(jnp.ones((Q, Q), bool))[:, :, None, None]
    lmat = jnp.exp(jnp.where(causal, acs[:, :, :, None] - acs[:, :, None, :], -jnp.inf))
    cb = jnp.einsum('bclgn,bcsgn->bclsg', cc, bc)
    y_diag = jnp.einsum('bclsg,bclsge,bcsgep->bclgep', cb, lmat, xs)
    decay_end = jnp.exp(acs[:, :, -1:] - acs)
    states = jnp.einsum('bcsgn,bcsge,bcsgep->bcgepn', bc, decay_end, xs)
    tot = jnp.exp(acs[:, :, -1])

    def step(h, inp):
        st, tt = inp
        return h * tt[..., None, None] + st, h

    h_fin, h_start = lax.scan(step, h0, (jnp.moveaxis(states, 1, 0), jnp.moveaxis(tot, 1, 0)))
    h_start = jnp.moveaxis(h_start, 0, 1)
    y_off = jnp.einsum('bclgn,bcgepn,bclge->bclgep', cc, h_start, jnp.exp(acs))
    return (y_diag + y_off).reshape(b, L, G, E, P), h_fin


def gdn_chunked(q, k, v, g, beta, s0):
    Q = GDN_CHUNK
    q, k, v, g, beta = (to_chunks(t, Q) for t in (q, k, v, g, beta))
    gc = jnp.cumsum(g, -1)
    idx = jnp.arange(Q)
    causal = idx[:, None] >= idx[None, :]
    strict = idx[:, None] > idx[None, :]
    dmask = jnp.exp(jnp.where(causal, gc[..., :, None] - gc[..., None, :], -jnp.inf))
    a_strict = jnp.where(strict, beta[..., :, None] * jnp.einsum('bchik,bchjk->bchij', k, k) * dmask, 0.0)
    m = a_strict + jnp.eye(Q, dtype=a_strict.dtype)
    rhs = jnp.concatenate([v * beta[..., None], k * (beta * jnp.exp(gc))[..., None]], -1)
    sol = lax.linalg.triangular_solve(m, rhs, left_side=True, lower=True, unit_diagonal=True)
    nv = v.shape[-1]
    u, w = sol[..., :nv], sol[..., nv:]
    attn = jnp.einsum('bchik,bchjk->bchij', q, k) * dmask
    q_dec = q * jnp.exp(gc)[..., None]
    k_dec = k * jnp.exp(gc[..., -1:] - gc)[..., None]
    o, s_fin = chunk_state_scan(s0, q_dec, k_dec, u, attn, jnp.exp(gc[..., -1]), w)
    return from_chunks(o), s_fin


def retention_chunked(q, k, v, log_gamma, s0):
    Q = RET_CHUNK
    b = q.shape[0]
    q, k, v = (to_chunks(t, Q) for t in (q, k, v))
    nc = q.shape[1]
    pos = jnp.arange(Q, dtype=jnp.float32)
    rel = pos[:, None] - pos[None, :]
    dmask = jnp.exp(jnp.where(rel >= 0, rel * log_gamma[:, None, None], -jnp.inf))
    attn = jnp.einsum('bchik,bchjk->bchij', q, k) * dmask
    q_dec = q * jnp.exp((pos + 1.0) * log_gamma[:, None])[..., None]
    k_dec = k * jnp.exp((Q - 1.0 - pos) * log_gamma[:, None])[..., None]
    tot = jnp.broadcast_to(jnp.exp(Q * log_gamma), (b, nc) + log_gamma.shape)
    o, s_fin = chunk_state_scan(s0, q_dec, k_dec, v, attn, tot)
    return from_chunks(o), s_fin


def grid_rope(L):
    rows = L // GRID_W
    r = jnp.repeat(jnp.arange(rows), GRID_W).astype(jnp.float32)
    col = jnp.tile(jnp.arange(GRID_W), rows).astype(jnp.float32)
    nf = RET_HEAD_DIM // 4
    inv = jnp.power(ROPE_BASE, -jnp.arange(nf, dtype=jnp.float32) / nf)
    ang = jnp.concatenate([r[:, None] * inv, col[:, None] * inv], -1)
    return jnp.cos(ang), jnp.sin(ang)


def apply_rope(x, cos, sin):
    half = x.shape[-1] // 2
    x1, x2 = x[..., :half], x[..., half:]
    c, s = cos[None, :, None], sin[None, :, None]
    return jnp.concatenate([x1 * c - x2 * s, x1 * s + x2 * c], -1)


def token_mixing(u, p, l, s_ssd, s_gdn, s_ret, rope):
    b, L, _ = u.shape
    f32 = jnp.float32
    proj = u @ p['w_in'][l]
    conv_out = dwconv(proj[..., :N_CONV], p['conv_w'][l], p['conv_b'][l])
    hy_v, hy_x1, hy_x2, s_x, s_b, s_c, g_q, g_k, g_v = split_sizes(conv_out, CONV_SIZES)
    s_z, s_dt, g_z, g_beta, g_a, r_q, r_k, r_v, r_g = split_sizes(proj[..., N_CONV:], REST_SIZES)

    y_hy = hyena_order2(hy_v, hy_x1, hy_x2, p['hy_w1'][l], p['hy_b1'][l], p['hy_w2'][l], p['hy_b2'][l],
                        p['hy_w3'][l], p['hy_freq'][l], p['hy_bias'][l])

    xh = jax.nn.silu(s_x.astype(f32)).reshape(b, L, SSD_GROUPS, SSD_HPG, SSD_HEAD_DIM)
    bm = jax.nn.silu(s_b.astype(f32)).reshape(b, L, SSD_GROUPS, SSD_STATE)
    cm = jax.nn.silu(s_c.astype(f32)).reshape(b, L, SSD_GROUPS, SSD_STATE)
    dt = jax.nn.softplus(s_dt.astype(f32).reshape(b, L, N_DIR, SSD_HEADS) + p['ssd_dt_bias'][l].astype(f32))
    a_rate = -jnp.exp(p['ssd_A_log'][l].astype(f32))
    d_skip = p['ssd_D'][l].astype(f32)
    ys, ssd_new = [], []
    for d in range(N_DIR):
        dt_d = dt[:, :, d].reshape(b, L, SSD_GROUPS, SSD_HPG)
        h0 = s_ssd[:, d].astype(f32).reshape(b, SSD_GROUPS, SSD_HPG, SSD_HEAD_DIM, SSD_STATE)
        y, hf = run_direction(ssd_chunked, d, (xh, dt_d, dt_d * a_rate[d].reshape(SSD_GROUPS, SSD_HPG), bm, cm), h0)
        ys.append(y + d_skip[d].reshape(SSD_GROUPS, SSD_HPG, 1) * xh)
        ssd_new.append(hf.reshape(b, SSD_HEADS, SSD_HEAD_DIM, SSD_STATE))
    y_ssd = (ys[0] + ys[1]).reshape(b, L, SSD_WIDTH).astype(u.dtype)
    y_ssd = rms_norm(y_ssd * jax.nn.silu(s_z), p['ssd_norm_w'][l])

    q = l2norm(jax.nn.silu(g_q.astype(f32)).reshape(b, L, GDN_HEADS, GDN_HEAD_DIM)) * GDN_HEAD_DIM ** -0.5
    k = l2norm(jax.nn.silu(g_k.astype(f32)).reshape(b, L, GDN_HEADS, GDN_HEAD_DIM))
    v = jax.nn.silu(g_v.astype(f32)).reshape(b, L, GDN_HEADS, GDN_HEAD_DIM)
    beta = jax.nn.sigmoid(g_beta.astype(f32).reshape(b, L, N_DIR, GDN_HEADS))
    g_log = -jnp.exp(p['gdn_A_log'][l].astype(f32)) * jax.nn.softplus(
        g_a.astype(f32).reshape(b, L, N_DIR, GDN_HEADS) + p['gdn_dt_bias'][l].astype(f32))
    outs, gdn_new = [], []
    for d in range(N_DIR):
        o, sf = run_direction(gdn_chunked, d, (q, k, v, g_log[:, :, d], beta[:, :, d]), s_gdn[:, d].astype(f32))
        outs.append(o)
        gdn_new.append(sf)
    o = (outs[0] + outs[1]).astype(u.dtype)
    y_gdn = (rms_norm(o, p['gdn_norm_w'][l]) *
             jax.nn.silu(g_z).reshape(b, L, GDN_HEADS, GDN_HEAD_DIM)).reshape(b, L, GDN_WIDTH)

    rq = r_q.astype(f32).reshape(b, L, RET_HEADS, RET_HEAD_DIM)
    rk = r_k.astype(f32).reshape(b, L, RET_HEADS, RET_HEAD_DIM) * RET_HEAD_DIM ** -0.5
    rv = r_v.astype(f32).reshape(b, L, RET_HEADS, RET_HEAD_DIM)
    if rope is not None:
        rq = apply_rope(rq, *rope)
        rk = apply_rope(rk, *rope)
    log_gamma = -jnp.exp(p['ret_decay'][l].astype(f32))
    outs, ret_new = [], []
    for d in range(N_DIR):
        o, sf = run_direction(retention_chunked, d, (rq, rk, rv), log_gamma[d], s_ret[:, d].astype(f32))
        outs.append(o)
        ret_new.append(sf)
    y_ret = norm_plain(outs[0] + outs[1]).astype(u.dtype).reshape(b, L, RET_WIDTH) * jax.nn.silu(r_g)

    out = jnp.concatenate([y_hy, y_ssd, y_gdn, y_ret], -1) @ p['w_out'][l]
    return out, (jnp.stack(ssd_new, 1), jnp.stack(gdn_new, 1), jnp.stack(ret_new, 1))


def conv_ffn(u, w_up, cw, cb, w_down):
    h = dwconv(u @ w_up, cw, cb)
    gate, val = jnp.split(h, 2, axis=-1)
    return (jax.nn.silu(gate) * val) @ w_down


def trunk_layer(x, cond, p, l, s_ssd, s_gdn, s_ret, rope):
    mod = jax.nn.silu(cond) @ p['w_mod'][l] + p['b_mod'][l]
    sh1, sc1, g1, sh2, sc2, g2 = jnp.split(mod[:, None, :], 6, axis=-1)
    y, states = token_mixing(x * (1.0 + sc1) + sh1, p, l, s_ssd, s_gdn, s_ret, rope)
    x = layer_norm(DN_ALPHA * x + g1 * y, p['ln1_g'][l], p['ln1_b'][l])
    f = conv_ffn(x * (1.0 + sc2) + sh2, p['w_up'][l], p['ffn_conv_w'][l], p['ffn_conv_b'][l], p['w_down'][l])
    x = layer_norm(DN_ALPHA * x + g2 * f, p['ln2_g'][l], p['ln2_b'][l])
    return x, states


def setup_inputs(seed: int = 0) -> dict:
    key = jax.random.key(seed)
    ks = iter(jax.random.split(key, 64))
    f32 = jnp.float32

    def nrm(shape, scale):
        return scale * jax.random.normal(next(ks), shape, f32)

    def unif(shape, lo, hi):
        return jax.random.uniform(next(ks), shape, f32, lo, hi)

    def dt_bias(shape):
        dt = jnp.exp(unif(shape, math.log(1e-3), math.log(1e-1)))
        return dt + jnp.log(-jnp.expm1(-dt))

    ret_base = jnp.asarray(np.log(-np.log(1.0 - 2.0 ** (-5.0 - np.arange(RET_HEADS)))), f32)
    return {
        'x_prompt': nrm((BATCH, SEQ, D_MODEL), 1.0),
        'x_sample': nrm((DEC_BATCH, DEC_SEQ, D_MODEL), 1.0),
        'state_ssd': nrm((DEC_BATCH, DEPTH, N_DIR, SSD_HEADS, SSD_HEAD_DIM, SSD_STATE), 0.1),
        'state_gdn': nrm((DEC_BATCH, DEPTH, N_DIR, GDN_HEADS, GDN_HEAD_DIM, GDN_HEAD_DIM), 0.3),
        'state_ret': nrm((DEC_BATCH, DEPTH, N_DIR, RET_HEADS, RET_HEAD_DIM, RET_HEAD_DIM), 1.0),
        'c': nrm((DEC_BATCH, D_MODEL), 1.0),
        'c_ctx': nrm((D_MODEL,), 1.0),
        'w_mod': nrm((DEPTH, D_MODEL, 6 * D_MODEL), 0.5 * D_MODEL ** -0.5),
        'b_mod': nrm((DEPTH, 6 * D_MODEL), 0.02),
        'w_in': nrm((DEPTH, D_MODEL, N_IN), D_MODEL ** -0.5),
        'conv_w': nrm((DEPTH, CONV_W, N_CONV), CONV_W ** -0.5),
        'conv_b': nrm((DEPTH, N_CONV), 0.02),
        'hy_w1': nrm((DEPTH, HY_EMB, HY_FFN), HY_EMB ** -0.5),
        'hy_b1': nrm((DEPTH, HY_FFN), 0.1),
        'hy_w2': nrm((DEPTH, HY_FFN, HY_FFN), HY_FFN ** -0.5),
        'hy_b2': nrm((DEPTH, HY_FFN), 0.1),
        'hy_w3': nrm((DEPTH, HY_FFN, HY_ORDER * N_DIR * HY_CH), HY_FFN ** -0.5),
        'hy_freq': 1.0 + nrm((DEPTH, HY_SINS, HY_FFN), 0.1),
        'hy_bias': nrm((DEPTH, HY_ORDER, HY_CH), 0.1),
        'ssd_A_log': jnp.log(unif((DEPTH, N_DIR, SSD_HEADS), 1.0, 16.0)),
        'ssd_dt_bias': dt_bias((DEPTH, N_DIR, SSD_HEADS)),
        'ssd_D': 1.0 + nrm((DEPTH, N_DIR, SSD_HEADS), 0.1),
        'ssd_norm_w': 1.0 + nrm((DEPTH, SSD_WIDTH), 0.02),
        'gdn_A_log': jnp.log(unif((DEPTH, N_DIR, GDN_HEADS), 1.0, 16.0)),
        'gdn_dt_bias': dt_bias((DEPTH, N_DIR, GDN_HEADS)),
        'gdn_norm_w': 1.0 + nrm((DEPTH, GDN_HEAD_DIM), 0.02),
        'ret_decay': ret_base + nrm((DEPTH, N_DIR, RET_HEADS), 0.05),
        'w_out': nrm((DEPTH, MIX_WIDTH, D_MODEL), DN_BETA * MIX_WIDTH ** -0.5),
        'ln1_g': 1.0 + nrm((DEPTH, D_MODEL), 0.02),
        'ln1_b': nrm((DEPTH, D_MODEL), 0.02),
        'w_up': nrm((DEPTH, D_MODEL, 2 * D_FF), D_MODEL ** -0.5),
        'ffn_conv_w': nrm((DEPTH, CONV_W, 2 * D_FF), CONV_W ** -0.5),
        'ffn_conv_b': nrm((DEPTH, 2 * D_FF), 0.02),
        'w_down': nrm((DEPTH, D_FF, D_MODEL), DN_BETA * D_FF ** -0.5),
        'ln2_g': 1.0 + nrm((DEPTH, D_MODEL), 0.02),
        'ln2_b': nrm((DEPTH, D_MODEL), 0.02),
    }


def reference(x_prompt, x_sample, state_ssd, state_gdn, state_ret, c, c_ctx, w_mod, b_mod, w_in, conv_w, conv_b,
              hy_w1, hy_b1, hy_w2, hy_b2, hy_w3, hy_freq, hy_bias, ssd_A_log, ssd_dt_bias, ssd_D, ssd_norm_w,
              gdn_A_log, gdn_dt_bias, gdn_norm_w, ret_decay, w_out, ln1_g, ln1_b, w_up, ffn_conv_w, ffn_conv_b,
              w_down, ln2_g, ln2_b):
    p = {'w_mod': w_mod, 'b_mod': b_mod, 'w_in': w_in, 'conv_w': conv_w, 'conv_b': conv_b,
         'hy_w1': hy_w1, 'hy_b1': hy_b1, 'hy_w2': hy_w2, 'hy_b2': hy_b2, 'hy_w3': hy_w3,
         'hy_freq': hy_freq, 'hy_bias': hy_bias, 'ssd_A_log': ssd_A_log, 'ssd_dt_bias': ssd_dt_bias,
         'ssd_D': ssd_D, 'ssd_norm_w': ssd_norm_w, 'gdn_A_log': gdn_A_log, 'gdn_dt_bias': gdn_dt_bias,
         'gdn_norm_w': gdn_norm_w, 'ret_decay': ret_decay, 'w_out': w_out, 'ln1_g': ln1_g, 'ln1_b': ln1_b,
         'w_up': w_up, 'ffn_conv_w': ffn_conv_w, 'ffn_conv_b': ffn_conv_b, 'w_down': w_down,
         'ln2_g': ln2_g, 'ln2_b': ln2_b}

    nb = x_prompt.shape[0]
    h = x_prompt
    ctx_cond = c_ctx[None, :]
    ssd_l, gdn_l, ret_l = [], [], []
    for l in range(DEPTH):
        z_ssd = jnp.zeros((nb, N_DIR, SSD_HEADS, SSD_HEAD_DIM, SSD_STATE), jnp.float32)
        z_gdn = jnp.zeros((nb, N_DIR, GDN_HEADS, GDN_HEAD_DIM, GDN_HEAD_DIM), jnp.float32)
        z_ret = jnp.zeros((nb, N_DIR, RET_HEADS, RET_HEAD_DIM, RET_HEAD_DIM), jnp.float32)
        h, (s1, s2, s3) = trunk_layer(h, ctx_cond, p, l, z_ssd, z_gdn, z_ret, None)
        ssd_l.append(s1)
        gdn_l.append(s2)
        ret_l.append(s3)
    y_prompt = h
    new_state_ssd = jnp.stack(ssd_l, 1)
    new_state_gdn = jnp.stack(gdn_l, 1)
    new_state_ret = jnp.stack(ret_l, 1)

    rope = grid_rope(x_sample.shape[1])
    h = x_sample
    for l in range(DEPTH):
        h, _ = trunk_layer(h, c, p, l, state_ssd[:, l], state_gdn[:, l], state_ret[:, l], rope)
    y_sample = h
    return (y_prompt, y_sample, new_state_ssd, new_state_gdn, new_state_ret)
```

```python
import math
import numpy as np
import ml_dtypes
from contextlib import ExitStack
import concourse.bass as bass
import concourse.mybir as mybir
from concourse.bass_utils import run_bass_kernel_spmd

F32 = mybir.dt.float32
BF16 = mybir.dt.bfloat16
AF = mybir.ActivationFunctionType
ALU = mybir.AluOpType
AX = mybir.AxisListType
DSIZE = {F32: 4, BF16: 2}
ENGS = ("pe", "dve", "act", "pool", "sp")
MULTIW = True


class Res:
    __slots__ = ("name", "last_w", "readers", "psum")

    def __init__(self, name, psum=False):
        self.name = name
        self.last_w = None
        self.readers = []
        self.psum = psum


class V:
    def __init__(self, ap, res):
        self.ap = ap
        self.res = res

    def __getitem__(self, idx):
        return V(self.ap[idx], self.res)

    def r(self, pattern, **kw):
        return V(self.ap.rearrange(pattern, **kw), self.res)

    def bc(self, dt):
        return V(self.ap.bitcast(dt), self.res)

    def pb(self, n):
        return V(self.ap.partition_broadcast(n), self.res)

    def tb(self, shape):
        return V(self.ap.to_broadcast(list(shape)), self.res)

    def us(self, axis):
        return V(self.ap.unsqueeze(axis), self.res)

    def part(self, idx, tag):
        return V(self.ap[idx], [Res(tag)])

    def newres(self, tag):
        return V(self.ap, [Res(tag)])

    @property
    def shape(self):
        return self.ap.shape


class Op:
    __slots__ = ("eng", "fn", "reads", "writes", "is_dma", "idx", "deps", "sem", "val", "needs_inc", "barrier")


class KB:
    def __init__(self, n_dma_sems=48, arena_bytes=211968):
        self.nc = bass.Bass("TRN2", target_bir_lowering=False)
        self.es = ExitStack()
        self.ops = []
        self.n_dma_sems = n_dma_sems
        self.arena_bytes = arena_bytes
        self.arena = self.es.enter_context(self.nc.sbuf_tensor("arena", [128, arena_bytes // 4], F32))[:]
        self.top = 0
        self.peak = 0
        self.banks = []
        for i in range(8):
            t = self.es.enter_context(self.nc.psum_tensor("bank%d" % i, [128, 512], F32))
            self.banks.append(V(t[:], [Res("bank%d" % i, psum=True)]))
        self.nres = 0

    def alloc(self, name, free_shape, dtype, parts=128):
        n = int(np.prod(free_shape)) * DSIZE[dtype]
        n = (n + 63) // 64 * 64
        off = self.top
        self.top += n
        self.peak = max(self.peak, self.top)
        assert self.top <= self.arena_bytes, "SBUF arena overflow at %s: %d" % (name, self.top)
        ap = self.arena[:, off // 4:(off + n) // 4]
        if dtype != F32:
            ap = ap.bitcast(dtype)
        nel = int(np.prod(free_shape))
        ap = ap[:, 0:nel]
        if len(free_shape) == 2:
            ap = ap.rearrange("p (a b) -> p a b", a=free_shape[0])
        elif len(free_shape) == 3:
            ap = ap.rearrange("p (a b c) -> p a b c", a=free_shape[0], b=free_shape[1])
        if parts != 128:
            ap = ap[0:parts]
        self.nres += 1
        return V(ap, [Res("%s#%d" % (name, self.nres))])

    def mark(self):
        return self.top

    def release(self, m):
        self.top = m
        self.barrier()

    def dram(self, name, shape, dtype, kind="Internal"):
        t = self.nc.dram_tensor(name, list(shape), dtype, kind=kind)
        return V(t.ap(), [Res(name)])

    def op(self, eng, fn, reads, writes, is_dma=False):
        o = Op()
        o.eng = eng
        o.fn = fn
        rr = []
        for v in reads:
            rr.extend(v.res)
        ww = []
        for v in writes:
            ww.extend(v.res)
        o.reads = rr
        o.writes = ww
        o.is_dma = is_dma
        o.idx = len(self.ops)
        o.needs_inc = is_dma
        o.barrier = False
        self.ops.append(o)
        return o

    def barrier(self):
        o = self.op("sp", None, [], [])
        o.barrier = True

    def dma(self, out, in_, eng="sp", **kw):
        return self.op(eng, lambda e: e.dma_start(out=out.ap, in_=in_.ap, **kw), [in_], [out], is_dma=True)

    def mm(self, out, lhsT, rhs, start=True, stop=True, **kw):
        return self.op("pe", lambda e: e.matmul(out.ap, lhsT.ap, rhs.ap, start=start, stop=stop, **kw),
                       [lhsT, rhs], [out])

    def tr(self, out, in_, ident):
        return self.op("pe", lambda e: e.transpose(out.ap, in_.ap, ident.ap), [in_, ident], [out])

    def act(self, out, in_, func, bias=None, scale=None, accum=None):
        reads = [in_]
        kw = {}
        if bias is not None:
            if isinstance(bias, V):
                reads.append(bias)
                kw["bias"] = bias.ap
            else:
                kw["bias"] = float(bias)
        if scale is not None:
            if isinstance(scale, V):
                reads.append(scale)
                kw["scale"] = scale.ap
            else:
                kw["scale"] = float(scale)
        writes = [out]
        if accum is not None:
            kw["accum_out"] = accum.ap
            writes.append(accum)
        return self.op("act", lambda e: e.activation(out.ap, in_.ap, func, **kw), reads, writes)

    def tt(self, out, a, b, op, eng="dve"):
        return self.op(eng, lambda e: e.tensor_tensor(out.ap, a.ap, b.ap, op), [a, b], [out])

    def ts(self, out, a, s1, op0, s2=None, op1=None, eng="dve", accum=None):
        reads = [a]

        def cv(s):
            if isinstance(s, V):
                reads.append(s)
                return s.ap
            return None if s is None else float(s)

        c1 = cv(s1)
        c2 = cv(s2)
        kw = {}
        writes = [out]
        if accum is not None:
            kw["accum_out"] = accum.ap
            writes.append(accum)
        if op1 is None:
            return self.op(eng, lambda e: e.tensor_scalar(out.ap, a.ap, c1, None, op0, **kw), reads, writes)
        return self.op(eng, lambda e: e.tensor_scalar(out.ap, a.ap, c1, c2, op0, op1, **kw), reads, writes)

    def stt(self, out, a, s, b, op0, op1, eng="dve"):
        reads = [a, b]
        if isinstance(s, V):
            reads.append(s)
            sv = s.ap
        else:
            sv = float(s)
        return self.op(eng, lambda e: e.scalar_tensor_tensor(out.ap, a.ap, sv, b.ap, op0, op1), reads, [out])

    def copy(self, out, in_, eng="dve"):
        if eng == "act":
            return self.op("act", lambda e: e.copy(out.ap, in_.ap), [in_], [out])
        return self.op(eng, lambda e: e.tensor_copy(out.ap, in_.ap), [in_], [out])

    def memset(self, out, val, eng="dve"):
        return self.op(eng, lambda e: e.memset(out.ap, val), [], [out])

    def recip(self, out, in_):
        return self.op("dve", lambda e: e.reciprocal(out.ap, in_.ap), [in_], [out])

    def scan(self, out, d0, d1, init, op0, op1):
        return self.op("dve", lambda e: e.tensor_tensor_scan(out.ap, d0.ap, d1.ap, init, op0, op1), [d0, d1], [out])

    def reduce(self, out, in_, op, axis=AX.X, eng="dve"):
        return self.op(eng, lambda e: e.tensor_reduce(out.ap, in_.ap, axis, op), [in_], [out])

    def _compress(self, idxs):
        ops = self.ops
        best = {}
        out = []
        for d in idxs:
            po = ops[d]
            if po.is_dma:
                out.append(d)
            elif po.eng not in best or best[po.eng] < d:
                best[po.eng] = d
        out.extend(best.values())
        return out

    def finalize(self):
        ops = self.ops
        nc = self.nc
        pending = {e: set() for e in ENGS}
        last_on = {e: None for e in ENGS}
        dmas_since = []
        for o in ops:
            if o.barrier:
                B = set(dmas_since)
                for e in ENGS:
                    if last_on[e] is not None:
                        B.add(last_on[e])
                for e in ENGS:
                    pending[e] |= B
                dmas_since = []
                o.deps = []
                continue
            deps = set()
            for r in o.reads:
                if r.last_w is not None:
                    deps.update(r.last_w)
                if r.psum:
                    for rd in r.readers:
                        if ops[rd].eng != o.eng:
                            deps.add(rd)
            for w in o.writes:
                if w.last_w is not None:
                    if not (MULTIW and o.is_dma and not w.readers and all(ops[x].is_dma for x in w.last_w)):
                        deps.update(w.last_w)
                deps.update(w.readers)
            for r in o.reads:
                r.readers.append(o.idx)
                if len(r.readers) > 64:
                    r.readers = self._compress(r.readers)
            for w in o.writes:
                if MULTIW and o.is_dma and w.last_w is not None and not w.readers and all(ops[x].is_dma for x in w.last_w):
                    w.last_w = w.last_w + [o.idx]
                else:
                    w.last_w = [o.idx]
                w.readers = []
            if pending[o.eng]:
                deps |= pending[o.eng]
                pending[o.eng] = set()
            deps.discard(o.idx)
            deps = self._compress(deps)
            o.deps = [d for d in deps
                      if not (o.eng == "pe" and ops[d].eng == "pe" and not ops[d].is_dma and not o.is_dma)]
            for d in o.deps:
                ops[d].needs_inc = True
            if o.is_dma:
                dmas_since.append(o.idx)
            else:
                last_on[o.eng] = o.idx
        with ExitStack() as es2:
            esem = {e: es2.enter_context(nc.semaphore("sem_" + e)) for e in ENGS}
            NSW = 24
            NT = self.n_dma_sems + NSW
            dsem = [es2.enter_context(nc.semaphore("dsem%d" % i)) for i in range(NT)]
            ecount = {e: 0 for e in ENGS}
            dcount = [0] * NT
            dlast = [None] * NT
            nd_hw = 0
            nd_sw = 0
            for o in ops:
                if o.barrier:
                    continue
                if o.is_dma:
                    if o.eng == "pool":
                        s = self.n_dma_sems + nd_sw % NSW
                        nd_sw += 1
                    else:
                        s = nd_hw % self.n_dma_sems
                        nd_hw += 1
                    if dlast[s] is not None:
                        o.deps.append(dlast[s])
                    dcount[s] += 16
                    o.sem = dsem[s]
                    o.val = dcount[s]
                    dlast[s] = o.idx
                elif o.needs_inc:
                    ecount[o.eng] += 1
                    o.sem = esem[o.eng]
                    o.val = ecount[o.eng]
            per_eng = {e: [] for e in ENGS}
            for o in ops:
                if not o.barrier:
                    per_eng[o.eng].append(o)
            self.stats = {e: len(per_eng[e]) for e in ENGS}
            self.stats["incs"] = dict(ecount)
            nwaits = [0]

            def emit(eng_name, e):
                seen = {}
                for o in per_eng[eng_name]:
                    need = {}
                    for d in o.deps:
                        po = ops[d]
                        key = id(po.sem)
                        if key not in need or need[key][1] < po.val:
                            need[key] = (po.sem, po.val)
                    for key, (sem, val) in need.items():
                        if seen.get(key, 0) >= val:
                            continue
                        e.wait_ge(sem, val)
                        nwaits[0] += 1
                        seen[key] = val
                    ins = o.fn(e)
                    if o.needs_inc:
                        ins.then_inc(o.sem, 16 if o.is_dma else 1)
                if eng_name == "sp":
                    for s in range(NT):
                        if dcount[s] > 0 and seen.get(id(dsem[s]), 0) < dcount[s]:
                            e.wait_ge(dsem[s], dcount[s])
                    for en in ENGS:
                        if en != "sp" and ecount[en] > 0:
                            e.wait_ge(esem[en], ecount[en])

            with nc.Block() as block:
                @block.sync
                def _(e):
                    emit("sp", e)

                @block.tensor
                def _(e):
                    emit("pe", e)

                @block.vector
                def _(e):
                    emit("dve", e)

                @block.scalar
                def _(e):
                    emit("act", e)

                @block.gpsimd
                def _(e):
                    emit("pool", e)
            self.stats["waits"] = nwaits[0]
            self.stats["sbuf_peak"] = self.peak
        self.es.close()
        return nc
DM = 2048
NTOK = 2560
NTT = 20
SEQS = ((0, 2048), (2048, 256), (2304, 256))
DFF = 5632
ALPHA = 4.0 ** 0.25
N_IN = 7200

def _tiles():
    T = []
    def add(name, col, n, conv, func, l2, fm, tm):
        for i in range(n):
            T.append((name, col + 128 * i, conv, func, l2, None if fm is None else fm + 128 * i,
                      None if tm is None else tm + 128 * i))
    add("hy_v", 0, 4, True, "id", None, None, 0)
    add("hy_x1", 512, 4, True, "id", None, None, 512)
    add("hy_x2", 1024, 4, True, "id", None, 0, None)
    add("s_x", 1536, 4, True, "silu", None, 512, 1024)
    add("s_b", 2048, 2, True, "silu", None, 1024, 1536)
    add("s_c", 2304, 2, True, "silu", None, 1280, None)
    add("g_q", 2560, 4, True, "silu", 128.0 ** -0.5, 1536, None)
    add("g_k", 3072, 4, True, "silu", 1.0, 2048, 1792)
    add("g_v", 3584, 4, True, "silu", None, None, 2304)
    add("s_z", 4096, 4, False, "silu", None, 2560, None)
    add("g_z", 4624, 4, False, "silu", None, 3072, None)
    add("r_g", 6688, 4, False, "silu", None, 3584, None)
    return T
FM_TILES = _tiles()
NFM = len(FM_TILES)
FM_ROWS = 4096
TM_COLS = 4352
TM_RQ, TM_RK, TM_RV = 2816, 3328, 3840
def _perm():
    p = []
    for t in FM_TILES:
        p.extend(range(t[1], t[1] + 128))
    p.extend(range(5152, 5152 + 1536))
    p.extend(range(4608, 4624))
    p.extend(range(5136, 5152))
    return np.array(p)
W_IN_PERM = _perm()
PP_CONV = 0
PP_FFN = 176
PP_SSDNW = 528
PP_GDNNW = 532
PP_HYB1, PP_HYF0, PP_HYB2, PP_HYF1 = 533, 534, 535, 536
PP_SSD_DTB = 537
PP_SSD_ALOG = 553
PP_SSD_D = 569
PP_GDN_DTB = 585
PP_GDN_ALOG = 593
PP_RET_DEC = 601
PP_SSD_DCOL = 612
NPP = 620
MOD_SH1, MOD_SC1, MOD_G1, MOD_SH2, MOD_SC2, MOD_G2 = [i * 2048 for i in range(6)]


def host_prep(inp, core):
    f32 = np.float32
    bf = ml_dtypes.bfloat16
    m = {}
    m["X0"] = np.ascontiguousarray(np.concatenate(
        [inp["x_sample"][core], inp["x_prompt"][2 * core], inp["x_prompt"][2 * core + 1]], 0))
    m["cond"] = np.ascontiguousarray(np.stack([inp["c"][core], inp["c_ctx"]], 0))
    m["st_ssd"] = np.ascontiguousarray(inp["state_ssd"][core])
    m["st_gdn"] = np.ascontiguousarray(inp["state_gdn"][core])
    m["st_ret"] = np.ascontiguousarray(inp["state_ret"][core])
    return m


def host_shared(inp):
    f32 = np.float32
    bf = ml_dtypes.bfloat16
    m = {}
    m["w_mod"] = inp["w_mod"]
    m["b_mod"] = inp["b_mod"]
    m["w_in"] = np.ascontiguousarray(inp["w_in"][:, :, W_IN_PERM])
    m["w_out"] = inp["w_out"]
    m["w_up"] = inp["w_up"]
    m["w_down"] = inp["w_down"]
    m["hy_w1"] = inp["hy_w1"]
    m["hy_w2"] = inp["hy_w2"]
    m["hy_w3"] = inp["hy_w3"]
    pp = np.zeros((2, 128, NPP), f32)
    for l in range(2):
        for i, t in enumerate(FM_TILES):
            if t[2]:
                ch = slice(t[1], t[1] + 128)
                pp[l, :, PP_CONV + 4 * i:PP_CONV + 4 * i + 3] = inp["conv_w"][l][:, ch].T
                pp[l, :, PP_CONV + 4 * i + 3] = inp["conv_b"][l][ch]
        for j in range(88):
            ch = slice(128 * j, 128 * j + 128)
            pp[l, :, PP_FFN + 4 * j:PP_FFN + 4 * j + 3] = inp["ffn_conv_w"][l][:, ch].T
            pp[l, :, PP_FFN + 4 * j + 3] = inp["ffn_conv_b"][l][ch]
        pp[l, :, PP_SSDNW:PP_SSDNW + 4] = inp["ssd_norm_w"][l].reshape(4, 128).T
        pp[l, :, PP_GDNNW] = inp["gdn_norm_w"][l]
        pp[l, :64, PP_HYB1] = inp["hy_b1"][l]
        pp[l, :64, PP_HYF0] = inp["hy_freq"][l, 0]
        pp[l, :64, PP_HYB2] = inp["hy_b2"][l]
        pp[l, :64, PP_HYF1] = inp["hy_freq"][l, 1]
        pp[l, :, PP_SSD_DTB:PP_SSD_DTB + 16] = inp["ssd_dt_bias"][l].reshape(1, 16)
        pp[l, :, PP_SSD_ALOG:PP_SSD_ALOG + 16] = inp["ssd_A_log"][l].reshape(1, 16)
        pp[l, :, PP_SSD_D:PP_SSD_D + 16] = inp["ssd_D"][l].reshape(1, 16)
        pp[l, :, PP_GDN_DTB:PP_GDN_DTB + 8] = inp["gdn_dt_bias"][l].reshape(1, 8)
        pp[l, :, PP_GDN_ALOG:PP_GDN_ALOG + 8] = inp["gdn_A_log"][l].reshape(1, 8)
        pp[l, :, PP_RET_DEC:PP_RET_DEC + 8] = inp["ret_decay"][l].reshape(1, 8)
        for d in range(2):
            for t in range(4):
                pp[l, 0:64, PP_SSD_DCOL + 4 * d + t] = inp["ssd_D"][l, d, 2 * t]
                pp[l, 64:128, PP_SSD_DCOL + 4 * d + t] = inp["ssd_D"][l, d, 2 * t + 1]
    m["pp"] = pp
    m["ln1_g"] = inp["ln1_g"]; m["ln1_b"] = inp["ln1_b"]; m["ln2_g"] = inp["ln2_g"]; m["ln2_b"] = inp["ln2_b"]
    m["hy_bias"] = np.ascontiguousarray(inp["hy_bias"].reshape(2, 1024))
    m["identb"] = np.eye(128).astype(bf)
    m["identf"] = np.eye(128).astype(f32)
    m["onesb"] = np.ones((128, 128)).astype(bf)
    m["onesf"] = np.ones((128, 128)).astype(f32)
    jj = np.arange(128)[:, None]
    ii = np.arange(128)[None, :]
    m["maskF"] = (jj <= ii).astype(f32)
    m["maskB"] = (jj >= ii).astype(f32)
    m["nmaskSF"] = -(jj < ii).astype(f32)
    m["nmaskSB"] = -(jj > ii).astype(f32)
    blk = np.zeros((14, 128, 128), f32)
    for lev in range(7):
        sz = 1 << lev
        mk = ((jj // (2 * sz)) == (ii // (2 * sz))) & ((jj // sz) % 2 == 1) & ((ii // sz) % 2 == 0)
        blk[lev] = mk
        blk[7 + lev] = mk.T
    m["blkM"] = blk.astype(bf)
    r = np.arange(2048) // 64
    col = np.arange(2048) % 64
    inv = np.power(10000.0, -np.arange(32, dtype=np.float64) / 32)
    ang = np.concatenate([r[:, None] * inv, col[:, None] * inv], -1)
    m["rope_cos"] = np.cos(ang).astype(f32)
    m["rope_sin"] = np.sin(ang).astype(f32)
    return m
FUNC = {"id": AF.Identity, "silu": AF.Silu}


def declare(k, dbg):
    D = {}
    def inp(name, shape, dt=F32):
        D[name] = k.dram(name, shape, dt, "ExternalInput")
    def scr(name, shape, dt, out=False):
        D[name] = k.dram(name, shape, dt, "ExternalOutput" if (out or name in dbg) else "Internal")
    inp("X0", [NTOK, DM]); inp("cond", [2, DM])
    inp("st_ssd", [2, 2, 8, 64, 128]); inp("st_gdn", [2, 2, 4, 128, 128]); inp("st_ret", [2, 2, 4, 128, 128])
    inp("w_mod", [2, DM, 6 * DM]); inp("b_mod", [2, 6 * DM]); inp("w_in", [2, DM, N_IN]); inp("w_out", [2, DM, DM])
    inp("w_up", [2, DM, 2 * DFF]); inp("w_down", [2, DFF, DM])
    inp("hy_w1", [2, 17, 64]); inp("hy_w2", [2, 64, 64]); inp("hy_w3", [2, 64, 2048])
    inp("pp", [2, 128, NPP])
    for n in ("ln1_g", "ln1_b", "ln2_g", "ln2_b"):
        inp(n, [2, DM])
    inp("hy_bias", [2, 1024])
    inp("identb", [128, 128], BF16); inp("identf", [128, 128]); inp("onesb", [128, 128], BF16); inp("onesf", [128, 128])
    inp("rope_cos", [2048, 64]); inp("rope_sin", [2048, 64])
    for n in ("maskF", "maskB", "nmaskSF", "nmaskSB"):
        inp(n, [128, 128])
    inp("blkM", [14, 128, 128], BF16)
    scr("modv", [2, 2, 6 * DM], F32)
    for l in range(2):
        scr("FM%d" % l, [FM_ROWS, NTOK], BF16)
        scr("TM%d" % l, [NTOK, TM_COLS], BF16)
        scr("SM%d" % l, [NTOK, 32], F32)
        scr("YT%d" % l, [DM, NTOK], BF16)
        scr("X1_%d" % l, [NTOK, DM], F32)
        scr("AT%d" % l, [DFF, NTOK], BF16)
        scr("F2_%d" % l, [NTOK, DM], F32)
    scr("X_1", [NTOK, DM], F32)
    scr("X_2", [NTOK, DM], F32, out=True)
    scr("ns_ssd", [2, 2, 2, 8, 64, 128], F32, out=True)
    scr("ns_gdn", [2, 2, 2, 4, 128, 128], F32, out=True)
    scr("ns_ret", [2, 2, 2, 4, 128, 128], F32, out=True)
    return D


def load_consts(k, D):
    C = {}
    for n, dt in (("identb", BF16), ("identf", F32), ("onesb", BF16), ("onesf", F32)):
        C[n] = k.alloc(n, [128], dt)
        k.dma(C[n], D[n])
    C["pp"] = [k.alloc("pp%d" % l, [NPP], F32) for l in range(2)]
    for l in range(2):
        k.dma(C["pp"][l], D["pp"][l])
    return C


def phase_mod(k, D, C):
    m = k.mark()
    cs = k.alloc("cs", [DM], F32, parts=2)
    k.dma(cs, D["cond"])
    k.act(cs, cs, AF.Silu)
    scT = k.alloc("scT", [16, 2], BF16)
    pt = k.banks[7]
    for kt in range(16):
        k.tr(pt[:, 2 * kt:2 * kt + 2], cs[:, kt * 128:(kt + 1) * 128], C["identf"][0:2, 0:2])
    k.copy(scT, pt[:, 0:32].r("p (a b) -> p a b", a=16))
    wb = [k.alloc("wmod%d" % i, [16, 512], BF16) for i in range(2)]
    bm = k.alloc("bmod", [6 * DM], F32, parts=2)
    res = k.alloc("modres", [6 * DM], F32, parts=2)
    cnt = 0
    for l in range(2):
        k.dma(bm, D["b_mod"][l, :].pb(2))
        wv = D["w_mod"][l].r("(k p) n -> p k n", p=128)
        for c in range(24):
            w = wb[cnt % 2]
            k.dma(w, wv[:, :, c * 512:(c + 1) * 512], eng="pool")
            bank = k.banks[cnt % 4]
            for kt in range(16):
                k.mm(bank[0:2, :], scT[:, kt, :], w[:, kt, :], start=(kt == 0), stop=(kt == 15))
            k.tt(res[:, c * 512:(c + 1) * 512], bank[0:2, :], bm[:, c * 512:(c + 1) * 512], ALU.add)
            cnt += 1
        k.dma(D["modv"][l], res)
    k.release(m)


def front(k, D, C, Xsrc, l, off_sc, off_sh, uT, uTb):
    m = k.mark()
    scb = [k.alloc("scb%d" % i, [DM], F32) for i in range(2)]
    shb = [k.alloc("shb%d" % i, [DM], F32) for i in range(2)]
    for ci in range(2):
        k.dma(scb[ci], D["modv"][l, ci, off_sc:off_sc + DM].pb(128))
        k.dma(shb[ci], D["modv"][l, ci, off_sh:off_sh + DM].pb(128))
        k.ts(scb[ci], scb[ci], 1.0, ALU.add)
    xb = [k.alloc("xb%d" % i, [DM], F32) for i in range(2)]
    ub = [k.alloc("ub%d" % i, [DM], BF16) for i in range(2)]
    for tt in range(NTT):
        ci = 0 if tt < 16 else 1
        x = xb[tt % 2]
        u = ub[tt % 2]
        k.dma(x, Xsrc[tt * 128:(tt + 1) * 128, :])
        k.tt(x, x, scb[ci], ALU.mult)
        k.tt(u, x, shb[ci], ALU.add, eng="pool")
        for half in range(2):
            bank = k.banks[6 + half].bc(BF16)
            for j in range(8):
                kt = half * 8 + j
                k.tr(bank[:, j * 128:(j + 1) * 128], u[:, kt * 128:(kt + 1) * 128], C["identb"])
            dst = V(uT.ap[:, half * 8:(half + 1) * 8, tt * 128:(tt + 1) * 128], uTb[tt // 4].res)
            k.copy(dst, bank.r("p (a b) -> p a b", a=8), eng=("act" if half == 0 else "dve"))
    k.release(m)


def seq_cols(tb):
    if tb < 4:
        return [(0, 512, 1 + 512 * tb, 512 * tb)]
    return [(0, 256, 2050, 2048), (256, 256, 2307, 2304)]


def conv_epilogue(k, banks5, cb, t1, wcol, eng2="dve"):
    for tb in range(5):
        for (pc, n, bc0, _t0) in seq_cols(tb):
            k.copy(cb[:, bc0:bc0 + n], banks5[tb][:, pc:pc + n], eng="act")
    k.ts(t1, cb[:, 0:2562], wcol[:, 0:1], ALU.mult)
    k.stt(t1, cb[:, 1:2563], wcol[:, 1:2], t1, ALU.mult, ALU.add)
    k.stt(t1, cb[:, 2:2564], wcol[:, 2:3], t1, ALU.mult, ALU.add, eng=eng2)


T1_SEGS = ((0, 2048, 0), (2049, 256, 2048), (2306, 256, 2304))


def phase_in(k, D, C, l, uT, uTb):
    m = k.mark()
    pp = C["pp"][l]
    FM, TM, SM = D["FM%d" % l], D["TM%d" % l], D["SM%d" % l]
    wv = D["w_in"][l].r("(k p) n -> p k n", p=128)
    wb = [k.alloc("win%d" % i, [16, 512], BF16) for i in range(2)]
    cb = [k.alloc("cb%d" % i, [2564], F32) for i in range(2)]
    for c in cb:
        for g in (0, 2049, 2306, 2563):
            k.memset(c[:, g:g + 1], 0.0)
    t1 = k.alloc("t1", [2562], F32)
    ob = [k.alloc("ob%d" % i, [NTOK], BF16) for i in range(2)]
    sf = k.alloc("sf", [NTOK], F32)
    sq = k.alloc("sq", [NTOK], BF16)
    tms = [k.alloc("tms%d" % i, [NTT, 128], BF16) for i in range(2)]
    nchunks = NFM // 4
    bcnt = 0
    ntm = 0

    def load_chunk(ci, ncols=512):
        k.dma(wb[ci % 2][:, :, 0:ncols], wv[:, :, ci * 512:ci * 512 + ncols], eng="pool")

    load_chunk(0)
    for ci in range(nchunks):
        if ci + 1 < nchunks + 3:
            load_chunk(ci + 1)
        w = wb[ci % 2]
        for mi in range(4):
            ti = ci * 4 + mi
            name, ocol, is_conv, func, l2, fm_row, tm_col = FM_TILES[ti]
            banks5 = []
            for tb in range(5):
                bank = k.banks[bcnt % 6]
                bcnt += 1
                for kt in range(16):
                    k.mm(bank, w[:, kt, mi * 128:(mi + 1) * 128], V(uT.ap[:, kt, tb * 512:(tb + 1) * 512], uTb[tb].res),
                         start=(kt == 0), stop=(kt == 15))
                banks5.append(bank)
            o = ob[ti % 2]
            pc = pp[:, PP_CONV + 4 * ti:PP_CONV + 4 * ti + 4]
            if is_conv:
                conv_epilogue(k, banks5, cb[ti % 2], t1, pc)
                dst = sf if l2 is not None else o
                for (c0, n, t0) in T1_SEGS:
                    k.act(dst[:, t0:t0 + n], t1[:, c0:c0 + n], FUNC[func], bias=pc[:, 3:4])
                if l2 is not None:
                    k.tt(sq, sf, sf, ALU.mult, eng="pool")
                    for tb in range(5):
                        bank = k.banks[6 + tb % 2]
                        k.mm(bank, C["onesb"], sq[:, tb * 512:(tb + 1) * 512])
                        rn = t1[:, tb * 512:(tb + 1) * 512]
                        k.act(rn, bank, AF.Sqrt, bias=1e-6)
                        k.recip(rn, rn)
                        k.stt(o[:, tb * 512:(tb + 1) * 512], rn, float(l2), sf[:, tb * 512:(tb + 1) * 512],
                              ALU.mult, ALU.mult)
            else:
                for tb in range(5):
                    k.act(o[:, tb * 512:(tb + 1) * 512], banks5[tb], FUNC[func])
            if fm_row is not None:
                k.dma(FM[fm_row:fm_row + 128, :], o)
            if tm_col is not None:
                ts_ = tms[ntm % 2]
                ntm += 1
                for g in range(3):
                    n8 = 8 if g < 2 else 4
                    bank = k.banks[6 + g % 2].bc(BF16)
                    for j in range(n8):
                        tt = g * 8 + j
                        k.tr(bank[:, j * 128:(j + 1) * 128], o[:, tt * 128:(tt + 1) * 128], C["identb"])
                    k.copy(ts_[:, g * 8:g * 8 + n8, :], bank[:, 0:n8 * 128].r("p (a b) -> p a b", a=n8),
                           eng=("act" if g == 1 else "dve"))
                k.dma(TM[:, tm_col:tm_col + 128].r("(t p) c -> p t c", p=128), ts_)
    stg = [k.alloc("stg%d" % i, [512], F32) for i in range(2)]
    stb = [k.alloc("stb%d" % i, [512], BF16) for i in range(2)]
    rc = [k.alloc("rc%d" % i, [64], F32) for i in range(2)]
    rs = [k.alloc("rs%d" % i, [64], F32) for i in range(2)]
    ra = k.alloc("ra", [4, 64], F32)
    rb = k.alloc("rb", [4, 64], F32)
    sms = [k.alloc("sms%d" % i, [32], F32) for i in range(2)]
    cnt = 0
    for j, (tmc, scale, rope) in enumerate(((TM_RQ, 1.0, True), (TM_RK, 128.0 ** -0.5, True), (TM_RV, 1.0, False))):
        ci = nchunks + j
        if j < 2:
            load_chunk(ci + 1)
        else:
            load_chunk(ci + 1, 32)
        w = wb[ci % 2]
        for tt in range(NTT):
            bank = k.banks[bcnt % 6]
            bcnt += 1
            for kt in range(16):
                k.mm(bank, V(uT.ap[:, kt, tt * 128:(tt + 1) * 128], uTb[tt // 4].res), w[:, kt, :],
                     start=(kt == 0), stop=(kt == 15))
            sb_ = stb[cnt % 2]
            if rope and tt < 16:
                sg = stg[cnt % 2]
                k.act(sg, bank, AF.Copy, scale=scale)
                c_, s_ = rc[cnt % 2], rs[cnt % 2]
                k.dma(c_, D["rope_cos"][tt * 128:(tt + 1) * 128, :])
                k.dma(s_, D["rope_sin"][tt * 128:(tt + 1) * 128, :])
                xv = sg.r("p (h two d) -> p h two d", h=4, two=2)
                ov = sb_.r("p (h two d) -> p h two d", h=4, two=2)
                cbv = c_.us(1).tb([128, 4, 64])
                sbv = s_.us(1).tb([128, 4, 64])
                k.tt(ra, xv[:, :, 0, :], cbv, ALU.mult)
                k.tt(rb, xv[:, :, 1, :], sbv, ALU.mult, eng="pool")
                k.tt(ov[:, :, 0, :], ra, rb, ALU.subtract)
                k.tt(ra, xv[:, :, 0, :], sbv, ALU.mult)
                k.tt(rb, xv[:, :, 1, :], cbv, ALU.mult, eng="pool")
                k.tt(ov[:, :, 1, :], ra, rb, ALU.add)
            else:
                k.act(sb_, bank, AF.Copy, scale=scale)
            k.dma(TM[tt * 128:(tt + 1) * 128, tmc:tmc + 512], sb_)
            cnt += 1
    w = wb[(nchunks + 3) % 2]
    for tt in range(NTT):
        bank = k.banks[bcnt % 6]
        bcnt += 1
        for kt in range(16):
            k.mm(bank[:, 0:32], V(uT.ap[:, kt, tt * 128:(tt + 1) * 128], uTb[tt // 4].res), w[:, kt, 0:32],
                 start=(kt == 0), stop=(kt == 15))
        s_ = sms[tt % 2]
        k.copy(s_, bank[:, 0:32])
        k.dma(SM[tt * 128:(tt + 1) * 128, :], s_)
    k.release(m)
class PsumPool:
    def __init__(self, k):
        self.k = k
        self.wide = [k.banks[i] for i in range(4)]
        self.small = []
        for r in range(4):
            for b in range(4, 8):
                self.small.append(k.banks[b][:, r * 128:(r + 1) * 128])
        self.wi = 0
        self.si = 0

    def w(self):
        v = self.wide[self.wi % 4]
        self.wi += 1
        return v

    def s(self):
        v = self.small[self.si % 16]
        self.si += 1
        return v


def load_masks(k, D):
    M = {}
    for n in ("maskF", "maskB", "nmaskSF", "nmaskSB"):
        M[n] = k.alloc(n, [128], F32)
        k.dma(M[n], D[n])
    M["blk"] = k.alloc("blk", [14, 128], BF16)
    k.dma(M["blk"], D["blkM"].r("a p c -> p a c"))
    return M


def la_mixer_gen(k, D, C, l, kind, si, own_scope=True):
    tok0, L = SEQS[si]
    NCH = L // 128
    sample = (si == 0)
    nh = 8 if kind == "ssd" else 4
    ssd, gdn, ret = kind == "ssd", kind == "gdn", kind == "ret"
    FM, TM, SM, YT = D["FM%d" % l], D["TM%d" % l], D["SM%d" % l], D["YT%d" % l]
    pp = C["pp"][l]
    m = k.mark() if own_scope else None
    PS = PsumPool(k)
    M = load_masks(k, D)
    identf, identb, onesf, onesb = C["identf"], C["identb"], C["onesf"], C["onesb"]
    tsl = slice(tok0, tok0 + L)

    def tm_load(name, col0, ncols):
        t = k.alloc(name, [NCH, ncols], BF16)
        k.dma(t, TM[tsl, col0:col0 + ncols].r("(c p) x -> p c x", p=128))
        return t

    def fm_load(name, row0, ntile):
        t = k.alloc(name, [ntile, L], BF16)
        for i in range(ntile):
            k.dma(t[:, i, :], FM[row0 + 128 * i:row0 + 128 * (i + 1), tsl])
        return t

    sm = k.alloc("sm", [NCH, 32], F32)
    k.dma(sm, SM[tsl, :].r("(c p) x -> p c x", p=128))
    if ssd:
        qT = fm_load("qT", 1280, 2)
        kT = fm_load("kT", 1024, 2)
        k_tm = tm_load("k_tm", 1536, 256)
        x_tm = tm_load("x_tm", 1024, 512)
    elif gdn:
        qT = fm_load("qT", 1536, 4)
        kT = fm_load("kT", 2048, 4)
        k_tm = tm_load("k_tm", 1792, 512)
        v_tm = tm_load("v_tm", 2304, 512)
    else:
        q_tm = tm_load("q_tm", TM_RQ, 512)
        k_tm = tm_load("k_tm", TM_RK, 512)
        v_tm = tm_load("v_tm", TM_RV, 512)
        qkT = [k.alloc("qkT%d" % i, [8, 128], BF16) for i in range(2)]
    nd = 2 * nh
    if ssd or gdn:
        dtb_col = PP_SSD_DTB if ssd else PP_GDN_DTB
        alog_col = PP_SSD_ALOG if ssd else PP_GDN_ALOG
        src0 = 0 if ssd else 24
        nAr = k.alloc("nAr", [nd], F32)
        k.act(nAr, pp[:, alog_col:alog_col + nd], AF.Exp)
        k.ts(nAr, nAr, -1.0, ALU.mult)
        dt = k.alloc("dt", [NCH, nd], F32)
        G = k.alloc("G", [NCH, nd], F32)
        k.tt(dt, sm[:, :, src0:src0 + nd], pp[:, dtb_col:dtb_col + nd].us(1).tb([128, NCH, nd]), ALU.add)
        k.act(dt, dt, AF.Exp)
        k.act(dt, dt, AF.Ln, bias=1.0)
        k.tt(G, dt, nAr.us(1).tb([128, NCH, nd]), ALU.mult)
        if gdn:
            beta = k.alloc("beta", [NCH, 8], F32)
            k.act(beta, sm[:, :, 16:24], AF.Sigmoid)
    else:
        Gc = k.alloc("Gc", [8], F32)
        k.act(Gc, pp[:, PP_RET_DEC:PP_RET_DEC + 8], AF.Exp)
        k.ts(Gc, Gc, -1.0, ALU.mult)
    acc = k.alloc("acc", [4, L], F32)
    visited = [False] * NCH
    Vd = 64 if ssd else 128
    SA = [k.alloc("SA%d" % d, [nh, Vd], F32) for d in range(2)]
    SbA = [k.alloc("SbA%d" % d, [nh, 128], BF16) for d in range(2)]
    S = [SA[i // nh][:, i % nh, :] for i in range(nd)]
    Sb = [SbA[i // nh][:, i % nh, :] for i in range(nd)]
    for i in range(nd):
        h = i % nh
        d = i // nh
        if ssd:
            k.memset(Sb[i], 0.0, eng="pool")
        lv = Sb[i][:, (h % 2) * 64:(h % 2) * 64 + 64] if ssd else Sb[i]
        if sample:
            if ssd:
                tmp = k.alloc("st_tmp%d" % i, [128], F32, parts=64)
                k.dma(tmp, D["st_ssd"][l, d, h])
                r = PS.s()
                k.tr(r[:, 0:64], tmp, identf[0:64, 0:64])
                k.copy(S[i], r[:, 0:64], eng="act")
            else:
                k.dma(S[i], D["st_gdn" if gdn else "st_ret"][l, d, h])
            k.copy(lv, S[i], eng="pool")
        else:
            k.memset(S[i], 0.0)
            if not ssd:
                k.memset(Sb[i], 0.0, eng="pool")
    W = []
    A = {}
    for d in range(2):
        for n in ("diff", "erow", "DTm"):
            A[(n, d)] = k.alloc("%sA%d" % (n, d), [nh, 128], F32)
        for n in ("qdT", "kdec", "attnT"):
            A[(n, d)] = k.alloc("%sA%d" % (n, d), [nh, 128], BF16)
        if gdn:
            A[("tmpA", d)] = k.alloc("tmpAA%d" % d, [nh, 128], F32)
            for n in ("Xp", "Ta", "TTa", "Tb", "TTb", "R", "val"):
                A[(n, d)] = k.alloc("%sA%d" % (n, d), [nh, 128], BF16)
            A[("vb", d)] = k.alloc("vbA%d" % d, [nh, 128], F32)
    for i in range(nd):
        w = {}
        d_, h_ = i // nh, i % nh
        for n in ("diff", "erow", "DTm", "qdT", "kdec", "attnT"):
            w[n] = A[(n, d_)][:, h_, :]
        if gdn:
            w["tmpA"] = A[("tmpA", d_)][:, h_, :]
            w["nbe"] = k.alloc("nbe%d" % i, [128], F32)
            w["NkTall"] = k.alloc("NkTall%d" % i, [7, 128], BF16)
            for n in ("Xp", "Ta", "TTa", "Tb", "TTb", "R", "val"):
                w[n] = A[(n, d_)][:, h_, :]
            for n in ("B0T", "kdTn"):
                w[n] = k.alloc("%s%d" % (n, i), [128], BF16)
        W.append(w)
    ncum = [k.alloc("ncum%d" % d, [nh], F32) for d in range(2)]
    rhsG = [k.alloc("rhsG%d" % i, [4, 128], F32) for i in range(2)]
    if gdn:
        rhsB = [k.alloc("rhsB%d" % i, [4, 128], F32) for i in range(2)]
        nbeta = k.alloc("nbeta", [NCH, 8], F32)
        k.ts(nbeta, beta, -1.0, ALU.mult)
    if ssd:
        valpad = [k.alloc("valpad%d" % d, [8, 128], BF16) for d in range(2)]
        for d in range(2):
            k.memset(valpad[d], 0.0, eng="pool")
    rcnt = 0
    ret_cached = False
    info = {}
    bank_of = {}
    bc_of = {}
    brow_of = {}
    yield
    for s in range(NCH):
        for d in range(2):
            c = s if d == 0 else NCH - 1 - s
            csl = slice(c * 128, (c + 1) * 128)
            mask = M["maskF"] if d == 0 else M["maskB"]
            nmaskS = M["nmaskSF"] if d == 0 else M["nmaskSB"]
            last = 127 if d == 0 else 0
            if ret and ret_cached:
                pass
            else:
                Gd = Gc[:, d * 4:(d + 1) * 4] if ret else G[:, c, d * nh:(d + 1) * nh]
                cr = PS.s()
                k.mm(cr[:, 0:nh], mask, Gd)
                k.act(ncum[d], cr[:, 0:nh], AF.Copy, scale=-1.0)
                for b in range(nh // 4):
                    rg = rhsG[rcnt % 2]
                    rcnt += 1
                    k.tt(rg, mask.us(1).tb([128, 4, 128]), Gd[:, b * 4:b * 4 + 4].us(2).tb([128, 4, 128]), ALU.mult, eng="pool")
                    bank = PS.w()
                    k.mm(bank, onesf, rg.r("p a b -> p (a b)"))
                    bank_of[(d, b)] = bank
                    for hh in range(4):
                        bc_of[(d, b * 4 + hh)] = bank[:, hh * 128:(hh + 1) * 128]
                if gdn:
                    rb_ = rhsB[d]
                    for hh in range(4):
                        k.ts(rb_[:, hh, :], identf, beta[:, c, d * 4 + hh:d * 4 + hh + 1], ALU.mult, eng="pool")
                    bank = PS.w()
                    k.mm(bank, onesf, rb_.r("p a b -> p (a b)"))
                    for hh in range(4):
                        brow_of[(d, hh)] = bank[:, hh * 128:(hh + 1) * 128]
            if ret:
                qk = qkT[(2 * s + d) % 2]
                bank = PS.w().bc(BF16)
                for hh in range(4):
                    k.tr(bank[:, hh * 128:(hh + 1) * 128], q_tm[:, c, hh * 128:(hh + 1) * 128], identb)
                    k.tr(bank[:, (4 + hh) * 128:(5 + hh) * 128], k_tm[:, c, hh * 128:(hh + 1) * 128], identb)
                k.copy(qk, bank.r("p (a b) -> p a b", a=8), eng="act")
            if ssd:
                xv = x_tm[:, c, :].r("p (t e x) -> p t e x", t=4, e=2)
                dv = dt[:, c, d * 8:(d + 1) * 8].r("p (t e) -> p t e", e=2)
                vp = valpad[d].r("p (t e) (f x) -> p t e f x", e=2, f=2)
                for e in range(2):
                    k.tt(vp[:, :, e, e, :], xv[:, :, e, :], dv[:, :, e].us(2).tb([128, 4, 64]), ALU.mult,
                         eng=("dve" if e == 0 else "pool"))
            for h in range(nh):
                i = d * nh + h
                if ssd:
                    qT_h = qT[:, h // 4, csl]
                    kT_h = kT[:, h // 4, csl]
                    ktm_h = k_tm[:, c, (h // 4) * 128:(h // 4 + 1) * 128]
                elif gdn:
                    qT_h = qT[:, h, csl]
                    kT_h = kT[:, h, csl]
                    ktm_h = k_tm[:, c, h * 128:(h + 1) * 128]
                else:
                    qT_h = qk[:, h, :]
                    kT_h = qk[:, 4 + h, :]
                    ktm_h = k_tm[:, c, h * 128:(h + 1) * 128]
                info[i] = dict(d=d, h=h, c=c, qT=qT_h, kT=kT_h, ktm=ktm_h, mask=mask, nmaskS=nmaskS, last=last,
                               qkbuf=(qk if ret else None))
        ALL = range(nd)
        if not (ret and ret_cached):
            for d in range(2):
                mask = M["maskF"] if d == 0 else M["maskB"]
                for b in range(nh // 4):
                    b4 = slice(b * 4, b * 4 + 4)
                    bank3 = bank_of[(d, b)].r("p (a x) -> p a x", a=4)
                    df = A[("diff", d)][:, b4, :]
                    k.tt(df, bank3, ncum[d][:, b4].us(2).tb([128, 4, 128]), ALU.add)
                    k.ts(df, df, 0.0, ALU.min)
                    k.act(df, df, AF.Exp)
                    k.act(A[("erow", d)][:, b4, :], bank3, AF.Exp)
                    k.tt(A[("DTm", d)][:, b4, :], df, mask.us(1).tb([128, 4, 128]), ALU.mult, eng="pool")
        for d in range(2):
            for b in range(nh // 4):
                kqb = k.banks[4 + (2 * d + b) % 4]
                for hh in range(4):
                    i = d * nh + b * 4 + hh
                    k.mm(kqb[:, hh * 128:(hh + 1) * 128], info[i]["kT"], info[i]["qT"])
                b4 = slice(b * 4, b * 4 + 4)
                k.tt(A[("attnT", d)][:, b4, :], kqb.r("p (a x) -> p a x", a=4), A[("DTm", d)][:, b4, :], ALU.mult)
        for d in range(2):
            c = s if d == 0 else NCH - 1 - s
            csl = slice(c * 128, (c + 1) * 128)
            last = 127 if d == 0 else 0
            if ssd:
                for g in range(2):
                    g4 = slice(g * 4, g * 4 + 4)
                    k.tt(A[("qdT", d)][:, g4, :], qT[:, g, csl].us(1).tb([128, 4, 128]), A[("erow", d)][:, g4, :], ALU.mult, eng="pool")
                    k.tt(A[("kdec", d)][:, g4, :], k_tm[:, c, g * 128:(g + 1) * 128].us(1).tb([128, 4, 128]),
                         A[("DTm", d)][:, g4, last:last + 1].tb([128, 4, 128]), ALU.mult, eng="pool")
            else:
                qsrc = qT[:, :, csl] if gdn else info[d * nh]["qkbuf"][:, 0:4, :]
                k.tt(A[("qdT", d)], qsrc, A[("erow", d)], ALU.mult, eng="pool")
                k.tt(A[("kdec", d)], k_tm[:, c, :].r("p (a x) -> p a x", a=4),
                     A[("DTm", d)][:, :, last:last + 1].tb([128, 4, 128]), ALU.mult, eng="pool")
        if gdn:
            kks = {}
            for i in ALL:
                kks[i] = PS.s()
                k.mm(kks[i], info[i]["kT"], info[i]["kT"])
            for i in ALL:
                k.tt(W[i]["tmpA"], kks[i], W[i]["diff"], ALU.mult)
            for d in range(2):
                nm = M["nmaskSF"] if d == 0 else M["nmaskSB"]
                k.tt(A[("tmpA", d)], A[("tmpA", d)], nm.us(1).tb([128, 4, 128]), ALU.mult, eng="pool")
            for i in ALL:
                f = info[i]
                brow = brow_of[(f["d"], f["h"])]
                k.tt(W[i]["B0T"], brow, W[i]["tmpA"], ALU.mult)
                k.tt(W[i]["nbe"], brow, W[i]["erow"], ALU.mult)
                k.stt(W[i]["kdTn"], W[i]["nbe"], -1.0, f["kT"], ALU.mult, ALU.mult)
            for i in ALL:
                m0 = 7 if info[i]["d"] == 0 else 0
                k.tt(W[i]["NkTall"], W[i]["B0T"].us(1).tb([128, 7, 128]), M["blk"][:, m0:m0 + 7, :], ALU.mult, eng="pool")
            Tc = {i: (identb, identb) for i in ALL}
            TcA = {0: None, 1: None}
            id3 = identb.us(1).tb([128, 4, 128])
            bcnt3 = 0
            for lev in range(7):
                for d in range(2):
                    p1b = k.banks[4 + bcnt3 % 4]
                    p2b = k.banks[4 + (bcnt3 + 1) % 4]
                    p3b = k.banks[4 + (bcnt3 + 2) % 4]
                    bcnt3 += 3
                    for hh in range(4):
                        i = d * 4 + hh
                        k.mm(p1b[:, hh * 128:(hh + 1) * 128], W[i]["NkTall"][:, lev, :], Tc[i][0])
                    k.copy(A[("Xp", d)].r("p a x -> p (a x)"), p1b, eng="act")
                    for hh in range(4):
                        i = d * 4 + hh
                        if lev < 6:
                            k.mm(p2b[:, hh * 128:(hh + 1) * 128], Tc[i][1], W[i]["Xp"])
                        k.mm(p3b[:, hh * 128:(hh + 1) * 128], W[i]["Xp"], Tc[i][1])
                    nT, nTT = ("Ta", "TTa") if lev % 2 == 0 else ("Tb", "TTb")
                    curT = id3 if TcA[d] is None else A[(TcA[d][0], d)]
                    curTT = id3 if TcA[d] is None else A[(TcA[d][1], d)]
                    if lev < 6:
                        k.tt(A[(nT, d)], p2b.r("p (a x) -> p a x", a=4), curT, ALU.add)
                    k.tt(A[(nTT, d)], p3b.r("p (a x) -> p a x", a=4), curTT, ALU.add)
                    TcA[d] = (nT, nTT)
                    for hh in range(4):
                        i = d * 4 + hh
                        Tc[i] = (W[i][nT], W[i][nTT])
            for i in ALL:
                W[i]["Pb"] = Tc[i][1]
        if ret:
            ret_cached = True
        vals = {}
        rbc = [0]

        def nbank():
            b = k.banks[4 + rbc[0] % 4]
            rbc[0] += 1
            return b

        if gdn:
            for d in range(2):
                c = s if d == 0 else NCH - 1 - s
                rb, vb_ = nbank(), nbank()
                for hh in range(4):
                    i = d * 4 + hh
                    k.mm(rb[:, hh * 128:(hh + 1) * 128], W[i]["kdTn"], Sb[i])
                k.tt(A[("vb", d)], v_tm[:, c, :].r("p (a x) -> p a x", a=4),
                     beta[:, c, d * 4:(d + 1) * 4].us(2).tb([128, 4, 128]), ALU.mult, eng="pool")
                k.tt(A[("R", d)], rb.r("p (a x) -> p a x", a=4), A[("vb", d)], ALU.add)
                for hh in range(4):
                    i = d * 4 + hh
                    k.mm(vb_[:, hh * 128:(hh + 1) * 128], W[i]["Pb"], W[i]["R"])
                k.copy(A[("val", d)].r("p a x -> p (a x)"), vb_, eng="act")
            for i in ALL:
                vals[i] = (W[i]["val"], W[i]["val"])
        elif ret:
            for i in ALL:
                f = info[i]
                vals[i] = (v_tm[:, f["c"], f["h"] * 128:(f["h"] + 1) * 128],) * 2
        else:
            for i in ALL:
                f = info[i]
                h = f["h"]
                vals[i] = (valpad[f["d"]][:, h, :], valpad[f["d"]][:, h, (h % 2) * 64:(h % 2) * 64 + 64])
        for d in range(2):
            c = s if d == 0 else NCH - 1 - s
            csl = slice(c * 128, (c + 1) * 128)
            last = 127 if d == 0 else 0
            ob = nbank()
            for t in range(4):
                hs = (2 * t, 2 * t + 1) if ssd else (t,)
                reg = ob[:, t * 128:(t + 1) * 128]
                for e, h in enumerate(hs):
                    i = d * nh + h
                    k.mm(reg, Sb[i], W[i]["qdT"], start=(e == 0), stop=False)
                    k.mm(reg, vals[i][0], W[i]["attnT"], start=False, stop=(e == len(hs) - 1))
            dst = acc[:, :, csl]
            ob3 = ob.r("p (a x) -> p a x", a=4)
            if not visited[c]:
                k.copy(dst, ob3, eng="act")
            else:
                k.tt(dst, ob3, dst, ALU.add)
            visited[c] = True
            snb = nbank()
            for h in range(nh):
                i = d * nh + h
                k.mm(snb[:, h * Vd:(h + 1) * Vd], W[i]["kdec"], vals[i][1])
            k.tt(SA[d], SA[d], A[("erow", d)][:, :, last:last + 1].tb([128, nh, Vd]), ALU.mult, eng="pool")
            k.tt(SA[d], snb.r("p (a x) -> p a x", a=nh), SA[d], ALU.add)
            if ssd:
                sbv = SbA[d].r("p (t e) (f x) -> p t e f x", e=2, f=2)
                sav = SA[d].r("p (t e) x -> p t e x", e=2)
                for e in range(2):
                    k.copy(sbv[:, :, e, e, :], sav[:, :, e, :], eng="pool")
            else:
                k.copy(SbA[d], SA[d], eng="pool")
        yield
    if not sample:
        pi = si - 1
        for i in range(nd):
            h = i % nh
            d = i // nh
            if ssd:
                r = PS.s()
                k.tr(r[0:64, :], S[i], identf)
                tmp = k.alloc("so_tmp%d" % i, [128], F32, parts=64)
                k.copy(tmp, r[0:64, :], eng="act")
                k.dma(D["ns_ssd"][pi, l, d, h], tmp)
            else:
                k.dma(D["ns_gdn" if gdn else "ns_ret"][pi, l, d, h], S[i])
    yrow0 = {"ssd": 512, "gdn": 1024, "ret": 1536}[kind]
    gate_row = {"ssd": 2560, "gdn": 3072, "ret": 3584}[kind]
    NB = (L + 511) // 512
    BW = min(512, L)
    gt = [k.alloc("gate%d" % i, [BW], BF16) for i in range(2)]
    yw = [k.alloc("yw%d" % i, [BW], F32) for i in range(4)]
    ysq = [k.alloc("ysq%d" % i, [BW], BF16) for i in range(2)]
    rn = k.alloc("rn", [BW], F32)
    yo = [k.alloc("yo%d" % i, [BW], BF16) for i in range(2)]
    if ssd:
        xg = [k.alloc("xg%d" % i, [BW], BF16) for i in range(2)]
        dsum = k.alloc("dsum", [4], F32)
        k.tt(dsum, pp[:, PP_SSD_DCOL:PP_SSD_DCOL + 4], pp[:, PP_SSD_DCOL + 4:PP_SSD_DCOL + 8], ALU.add)
    cnt = 0
    for nb in range(NB):
        bsl = slice(nb * BW, (nb + 1) * BW)
        gsl = slice(tok0 + nb * BW, tok0 + (nb + 1) * BW)
        if ssd:
            ssb = PS.w()
            for t in range(4):
                g_ = gt[cnt % 2]
                x_ = xg[cnt % 2]
                cnt += 1
                k.dma(g_, FM[gate_row + 128 * t:gate_row + 128 * (t + 1), gsl])
                k.dma(x_, FM[512 + 128 * t:512 + 128 * (t + 1), gsl])
                y = yw[t]
                k.stt(y, x_, dsum[:, t:t + 1], acc[:, t, bsl], ALU.mult, ALU.add)
                k.tt(y, y, g_, ALU.mult, eng="pool")
                sq_ = ysq[t % 2]
                k.tt(sq_, y, y, ALU.mult, eng="pool")
                k.mm(ssb[:, 0:BW], onesb, sq_, start=(t == 0), stop=(t == 3))
            k.act(rn, ssb[:, 0:BW], AF.Sqrt, bias=1e-6, scale=1.0 / 512)
            k.recip(rn, rn)
            for t in range(4):
                o_ = yo[t % 2]
                k.stt(o_, yw[t], pp[:, PP_SSDNW + t:PP_SSDNW + t + 1], rn, ALU.mult, ALU.mult)
                k.dma(YT[yrow0 + 128 * t:yrow0 + 128 * (t + 1), gsl], o_)
        else:
            for t in range(4):
                g_ = gt[cnt % 2]
                cnt += 1
                k.dma(g_, FM[gate_row + 128 * t:gate_row + 128 * (t + 1), gsl])
                a_ = acc[:, t, bsl]
                y = yw[t % 2]
                if ret:
                    mb = PS.w()
                    k.mm(mb[:, 0:BW], onesf, a_)
                    k.stt(y, mb[:, 0:BW], -1.0 / 128, a_, ALU.mult, ALU.add)
                    src = y
                else:
                    src = a_
                sq_ = ysq[t % 2]
                k.tt(sq_, src, src, ALU.mult, eng="pool")
                ssb = PS.w()
                k.mm(ssb[:, 0:BW], onesb, sq_)
                k.act(rn, ssb[:, 0:BW], AF.Sqrt, bias=(1e-5 if ret else 1e-6), scale=1.0 / 128)
                k.recip(rn, rn)
                o_ = yo[t % 2]
                if ret:
                    k.tt(y, src, rn, ALU.mult)
                    k.tt(o_, y, g_, ALU.mult, eng="pool")
                else:
                    k.stt(y, src, pp[:, PP_GDNNW:PP_GDNNW + 1], rn, ALU.mult, ALU.mult)
                    k.tt(o_, y, g_, ALU.mult, eng="pool")
                k.dma(YT[yrow0 + 128 * t:yrow0 + 128 * (t + 1), gsl], o_)
    if own_scope:
        k.release(m)
    yield


def la_mixer(k, D, C, l, kind, si):
    for _ in la_mixer_gen(k, D, C, l, kind, si):
        pass


def la_mixer_pair(k, D, C, l, specs):
    m = k.mark()
    gens = [la_mixer_gen(k, D, C, l, kind, si, own_scope=False) for (kind, si) in specs]
    live = list(gens)
    while live:
        for g in list(live):
            try:
                next(g)
            except StopIteration:
                live.remove(g)
    k.release(m)
I32 = mybir.dt.int32
DSIZE[I32] = 4
TWO_PI = 2.0 * math.pi


def hyena_consts(L):
    bf = ml_dtypes.bfloat16
    N = 2 * L
    LT = L // 128
    FT = N // 128
    n = np.arange(L, dtype=np.float64)
    f = np.arange(L, dtype=np.float64)
    ang = 2.0 * np.pi * np.outer(n, f) / N
    Fk = np.zeros((N, N), np.float64)
    Fk[:L, :L] = np.cos(ang)
    Fk[:L, L] = np.cos(np.pi * n)
    Fk[:L, L + 1:] = -np.sin(ang[:, 1:])
    Fk[L:, :L] = np.cos(ang)
    Fk[L:, L] = np.cos(np.pi * n)
    Fk[L:, L + 1:] = np.sin(ang[:, 1:])
    Fk[L, :] = 0.0
    FkTt = Fk.reshape(FT, 128, FT, 128).transpose(2, 1, 0, 3)
    t = np.arange(L, dtype=np.float64)
    G = np.zeros((N, L), np.float64)
    angt = 2.0 * np.pi * np.outer(f, t) / N
    G[:L, :] = (2.0 / N) * np.cos(angt)
    G[0, :] = 1.0 / N
    G[L, :] = (1.0 / N) * np.cos(np.pi * t)
    G[L + 1:, :] = -(2.0 / N) * np.sin(angt[1:, :])
    GTa = G.reshape(FT, 128, LT, 128).transpose(2, 1, 0, 3)
    BW = min(512, L)
    GTb = G.reshape(FT, 128, L // BW, BW).transpose(2, 1, 0, 3)
    pos = np.arange(L, dtype=np.float32)
    tn = pos / np.float32(L - 1)
    bands = np.linspace(1e-4, 7, 8, dtype=np.float32)
    a2 = (np.float32(2.0 * math.pi / L) * pos[:, None]) * bands
    feats = np.concatenate([tn[:, None], np.cos(a2), -np.sin(a2)], -1).astype(np.float32)
    deltas = np.abs(np.linspace(math.log(1e-2) / 1.5, math.log(1e-2) / 0.3, 512, dtype=np.float32))
    dec = np.exp(-tn[:, None] * deltas).astype(np.float32)
    return {"FkTt%d" % L: np.ascontiguousarray(FkTt).astype(bf), "GTa%d" % L: np.ascontiguousarray(GTa).astype(bf),
            "GTb%d" % L: np.ascontiguousarray(GTb).astype(bf), "featsT%d" % L: np.ascontiguousarray(feats.T),
            "dec%d" % L: dec}


def declare_hyena(k, D, dbg):
    for L in (2048, 256):
        LT, FT, BW = L // 128, 2 * L // 128, min(512, L)
        D["FkTt%d" % L] = k.dram("FkTt%d" % L, [FT, 128, FT, 128], BF16, "ExternalInput")
        D["GTa%d" % L] = k.dram("GTa%d" % L, [LT, 128, FT, 128], BF16, "ExternalInput")
        D["GTb%d" % L] = k.dram("GTb%d" % L, [L // BW, 128, FT, BW], BF16, "ExternalInput")
        D["featsT%d" % L] = k.dram("featsT%d" % L, [17, L], F32, "ExternalInput")
        D["dec%d" % L] = k.dram("dec%d" % L, [L, 512], F32, "ExternalInput")
        for l in range(2):
            n = "KH%d_%d" % (l, L)
            D[n] = k.dram(n, [2 * L, 1024], F32, "ExternalOutput" if n in dbg else "Internal")


def hyena_filter(k, D, C, l, L):
    LT, FT = L // 128, 2 * L // 128
    NB = max(1, L // 512)
    BW = min(512, L)
    pp = C["pp"][l]
    m = k.mark()
    PS = PsumPool(k)
    w1 = k.alloc("hw1", [64], F32, parts=17)
    w2 = k.alloc("hw2", [64], F32, parts=64)
    w3 = k.alloc("hw3", [2048], F32, parts=64)
    fT = k.alloc("fT", [L], F32, parts=17)
    k.dma(w1, D["hy_w1"][l]); k.dma(w2, D["hy_w2"][l]); k.dma(w3, D["hy_w3"][l]); k.dma(fT, D["featsT%d" % L])
    hT = [k.alloc("hT%d" % i, [L], F32, parts=64) for i in range(2)]
    fb = k.alloc("fb", [2], F32, parts=64)
    k.tt(fb[:, 0:1], pp[0:64, PP_HYF0:PP_HYF0 + 1], pp[0:64, PP_HYB1:PP_HYB1 + 1], ALU.mult)
    k.tt(fb[:, 1:2], pp[0:64, PP_HYF1:PP_HYF1 + 1], pp[0:64, PP_HYB2:PP_HYB2 + 1], ALU.mult)
    arg = k.alloc("harg", [BW], F32, parts=64)
    ni = k.alloc("hni", [BW], I32, parts=64)
    nf = k.alloc("hnf", [BW], F32, parts=64)
    for layer in range(2):
        fcol = PP_HYF0 if layer == 0 else PP_HYF1
        for nb in range(NB):
            bsl = slice(nb * BW, (nb + 1) * BW)
            bank = PS.w()
            if layer == 0:
                k.mm(bank[0:64, 0:BW], w1, fT[:, bsl])
            else:
                k.mm(bank[0:64, 0:BW], w2, hT[0][:, bsl])
            k.ts(arg, bank[0:64, 0:BW], pp[0:64, fcol:fcol + 1], ALU.mult, fb[:, layer:layer + 1], ALU.add)
            k.ts(nf, arg, 1.0 / TWO_PI, ALU.mult)
            k.copy(ni, nf)
            k.copy(nf, ni)
            k.stt(arg, nf, -TWO_PI, arg, ALU.mult, ALU.add)
            k.act(hT[layer][:, bsl], arg, AF.Sin)
    h2T = hT[1]
    kernb = k.alloc("kernb", [2 * LT, 1024], BF16)
    dect = [k.alloc("dect%d" % i, [512], F32) for i in range(2)]
    absb = [k.alloc("absb%d" % i, [512], BF16) for i in range(2)]
    Sb0, Sb1 = PS.w(), PS.w()
    cnt = 0
    for tt in range(LT):
        dc = dect[tt % 2]
        k.dma(dc, D["dec%d" % L][tt * 128:(tt + 1) * 128, :])
        for cbk in range(4):
            o, d = cbk // 2, cbk % 2
            p = PS.s() if False else k.banks[4 + cnt % 4]
            k.mm(p, h2T[:, tt * 128:(tt + 1) * 128], w3[:, cbk * 512:(cbk + 1) * 512])
            dst = kernb[:, d * LT + tt, o * 512:(o + 1) * 512]
            k.tt(dst, p, dc, ALU.mult)
            if d == 1 and tt == 0:
                k.memset(kernb[0:1, LT, o * 512:(o + 1) * 512], 0.0)
            ab = absb[cnt % 2]
            k.act(ab, dst, AF.Abs)
            first = (tt == 0 and d == 0)
            lastt = (tt == LT - 1 and d == 1)
            k.mm(Sb0 if o == 0 else Sb1, C["onesb"], ab, start=first, stop=lastt)
            cnt += 1
    sinv = k.alloc("sinv", [1024], F32)
    k.recip(sinv[:, 0:512], Sb0)
    k.recip(sinv[:, 512:1024], Sb1)
    hyb = k.alloc("hyb", [1024], F32)
    k.dma(hyb, D["hy_bias"][l, :].pb(128))
    fk = [k.alloc("fk%d" % i, [2 * LT, 128], BF16) for i in range(2)]
    kh = [k.alloc("kh%d" % i, [1024], F32) for i in range(2)]
    KH = D["KH%d_%d" % (l, L)]
    for pt in range(FT):
        f_ = fk[pt % 2]
        k.dma(f_, D["FkTt%d" % L][pt])
        b0, b1 = k.banks[(2 * pt) % 4], k.banks[(2 * pt + 1) % 4]
        for lt in range(2 * LT):
            k.mm(b0, f_[:, lt, :], kernb[:, lt, 0:512], start=(lt == 0), stop=(lt == 2 * LT - 1))
        for lt in range(2 * LT):
            k.mm(b1, f_[:, lt, :], kernb[:, lt, 512:1024], start=(lt == 0), stop=(lt == 2 * LT - 1))
        kk_ = kh[pt % 2]
        k.tt(kk_[:, 0:512], b0, sinv[:, 0:512], ALU.mult)
        k.tt(kk_[:, 512:1024], b1, sinv[:, 512:1024], ALU.mult)
        if pt < FT // 2:
            k.tt(kk_, kk_, hyb, ALU.add, eng="pool")
        elif pt == FT // 2:
            k.tt(kk_[0:1, :], kk_[0:1, :], hyb[0:1, :], ALU.add, eng="pool")
        k.dma(KH[pt * 128:(pt + 1) * 128, :], kk_)
    k.release(m)


def hyena_conv(k, D, C, l, si):
    tok0, L = SEQS[si]
    LT, FT = L // 128, 2 * L // 128
    BW = min(512, L)
    NB = L // BW
    FM, TM, YT = D["FM%d" % l], D["TM%d" % l], D["YT%d" % l]
    KH = D["KH%d_%d" % (l, L)]
    m = k.mark()
    tsl = slice(tok0, tok0 + L)
    v_tm = k.alloc("hv", [LT, 512], BF16)
    x1_tm = k.alloc("hx1", [LT, 512], BF16)
    z1 = k.alloc("hz1", [LT, 512], BF16)
    k.dma(v_tm, TM[tsl, 0:512].r("(c p) x -> p c x", p=128))
    k.dma(x1_tm, TM[tsl, 512:1024].r("(c p) x -> p c x", p=128))
    Y = k.alloc("hY", [FT, 512], BF16)
    fkr = [k.alloc("fkr%d" % i, [LT, 128], BF16) for i in range(2)]
    fki = [k.alloc("fki%d" % i, [LT, 128], BF16) for i in range(2)]
    Kr = [k.alloc("Kr%d" % i, [512], F32) for i in range(2)]
    Ki = [k.alloc("Ki%d" % i, [512], F32) for i in range(2)]
    zr = k.alloc("zr", [512], F32)
    zi = k.alloc("zi", [512], F32)
    t1 = k.alloc("ht1", [512], F32)
    t2 = k.alloc("ht2", [512], F32)
    t3 = k.alloc("ht3", [512], F32)
    t4 = k.alloc("ht4", [512], F32)
    H = FT // 2
    cnt = 0
    for order in range(2):
        z = v_tm if order == 0 else z1
        osl = slice(order * 512, (order + 1) * 512)
        for pq in range(H):
            fr, fi = fkr[cnt % 2], fki[cnt % 2]
            kr, ki = Kr[cnt % 2], Ki[cnt % 2]
            cnt += 1
            k.dma(fr, D["FkTt%d" % L][pq][:, 0:LT, :])
            k.dma(fi, D["FkTt%d" % L][pq + H][:, 0:LT, :])
            k.dma(kr, KH[pq * 128:(pq + 1) * 128, osl])
            k.dma(ki, KH[(pq + H) * 128:(pq + H + 1) * 128, osl])
            br, bi = k.banks[(2 * pq) % 4], k.banks[(2 * pq + 1) % 4]
            for lt in range(LT):
                k.mm(br, fr[:, lt, :], z[:, lt, :], start=(lt == 0), stop=(lt == LT - 1))
            for lt in range(LT):
                k.mm(bi, fi[:, lt, :], z[:, lt, :], start=(lt == 0), stop=(lt == LT - 1))
            k.copy(zr, br, eng="act")
            k.copy(zi, bi, eng="act")
            k.tt(t1, zr, kr, ALU.mult)
            k.tt(t2, zi, ki, ALU.mult, eng="pool")
            k.tt(Y[:, pq, :], t1, t2, ALU.subtract)
            k.tt(t3, zr, ki, ALU.mult)
            k.tt(t4, zi, kr, ALU.mult, eng="pool")
            k.tt(Y[:, pq + H, :], t3, t4, ALU.add, eng="pool")
            if pq == 0:
                k.tt(Y[0:1, 0, :], zr[0:1, :], kr[0:1, :], ALU.mult)
                k.tt(Y[0:1, H, :], zi[0:1, :], ki[0:1, :], ALU.mult)
        if order == 0:
            ga = [k.alloc("ga%d" % i, [FT, 128], BF16) for i in range(2)] if si >= 0 else None
            for tt in range(LT):
                g_ = ga[tt % 2]
                k.dma(g_, D["GTa%d" % L][tt])
                bank = k.banks[4 + tt % 4]
                for pt in range(FT):
                    k.mm(bank, g_[:, pt, :], Y[:, pt, :], start=(pt == 0), stop=(pt == FT - 1))
                k.tt(z1[:, tt, :], bank, x1_tm[:, tt, :], ALU.mult)
        else:
            gb = k.alloc("gb", [FT, BW], BF16)
            x2 = [k.alloc("hx2%d" % i, [BW], BF16) for i in range(2)]
            yo = [k.alloc("hyo%d" % i, [BW], BF16) for i in range(2)]
            c2 = 0
            for nb in range(NB):
                k.dma(gb, D["GTb%d" % L][nb])
                gsl = slice(tok0 + nb * BW, tok0 + (nb + 1) * BW)
                for ct in range(4):
                    bank = k.banks[4 + c2 % 4]
                    x_, o_ = x2[c2 % 2], yo[c2 % 2]
                    c2 += 1
                    k.dma(x_, FM[ct * 128:(ct + 1) * 128, gsl])
                    for pt in range(FT):
                        k.mm(bank[:, 0:BW], Y[:, pt, ct * 128:(ct + 1) * 128], gb[:, pt, :], start=(pt == 0), stop=(pt == FT - 1))
                    k.tt(o_, bank[:, 0:BW], x_, ALU.mult)
                    k.dma(YT[ct * 128:(ct + 1) * 128, gsl], o_)
    k.release(m)
def ln_tile(k, z, lng, lnb, out, st, junk):
    k.act(junk, z, AF.Identity, accum=st[:, 0:1])
    k.ts(st[:, 1:2], st[:, 0:1], -1.0 / DM, ALU.mult)
    k.ts(z, z, st[:, 1:2], ALU.add)
    k.act(junk, z, AF.Square, accum=st[:, 2:3])
    k.act(st[:, 3:4], st[:, 2:3], AF.Sqrt, bias=1e-5, scale=1.0 / DM)
    k.recip(st[:, 3:4], st[:, 3:4])
    k.stt(z, z, st[:, 3:4], lng, ALU.mult, ALU.mult)
    k.tt(out, z, lnb, ALU.add, eng="pool")


def phase_out(k, D, C, l, Xsrc):
    m = k.mark()
    YT, X1 = D["YT%d" % l], D["X1_%d" % l]
    wout = k.alloc("wout", [16, DM], BF16)
    wv = D["w_out"][l].r("(k p) n -> p k n", p=128)
    woutc = [wout.part((slice(None), slice(None), slice(cc * 512, (cc + 1) * 512)), "woutc%d_%d" % (cc, l)) for cc in range(4)]
    for cc in range(4):
        k.dma(woutc[cc], wv[:, :, cc * 512:(cc + 1) * 512], eng="pool")
    ytb = [k.alloc("ytb%d" % i, [16, 512], BF16) for i in range(2)]
    xb = [k.alloc("oxb%d" % i, [DM], F32) for i in range(2)]
    zb = [k.alloc("ozb%d" % i, [DM], F32) for i in range(2)]
    junk = k.alloc("ojunk", [DM], BF16)
    g1 = k.alloc("g1", [DM], F32)
    lng = k.alloc("lng", [DM], F32)
    lnb = k.alloc("lnb", [DM], F32)
    st = [k.alloc("ost%d" % i, [4], F32) for i in range(2)]
    k.dma(lng, D["ln1_g"][l, :].pb(128))
    k.dma(lnb, D["ln1_b"][l, :].pb(128))
    ytv = YT.r("(k p) t -> p k t", p=128)
    for tb in range(5):
        y = ytb[tb % 2]
        k.dma(y, ytv[:, :, tb * 512:(tb + 1) * 512])
        if tb == 0 or tb == 4:
            k.dma(g1, D["modv"][l, 0 if tb == 0 else 1, MOD_G1:MOD_G1 + DM].pb(128))
        for j in range(4):
            tt = tb * 4 + j
            x = xb[tt % 2]
            z = zb[tt % 2]
            k.dma(x, Xsrc[tt * 128:(tt + 1) * 128, :])
            for cc in range(4):
                bank = k.banks[(tt % 2) * 4 + cc]
                for kt in range(16):
                    k.mm(bank, y[:, kt, j * 128:(j + 1) * 128], woutc[cc][:, kt, :],
                         start=(kt == 0), stop=(kt == 15))
                k.tt(z[:, cc * 512:(cc + 1) * 512], bank, g1[:, cc * 512:(cc + 1) * 512], ALU.mult)
            k.stt(z, x, ALPHA, z, ALU.mult, ALU.add)
            ln_tile(k, z, lng, lnb, x, st[tt % 2], junk)
            k.dma(X1[tt * 128:(tt + 1) * 128, :], x)
    k.release(m)


def phase_up(k, D, C, l, uT, uTb):
    m = k.mark()
    pp = C["pp"][l]
    AT = D["AT%d" % l]
    wv = D["w_up"][l].r("(k p) n -> p k n", p=128)
    wg = [k.alloc("wg%d" % i, [16, 256], BF16) for i in range(2)]
    wvl = [k.alloc("wvl%d" % i, [16, 256], BF16) for i in range(2)]
    cb = [k.alloc("ucb%d" % i, [2564], F32) for i in range(2)]
    for c in cb:
        for g in (0, 2049, 2306, 2563):
            k.memset(c[:, g:g + 1], 0.0)
    t1g = k.alloc("t1g", [2562], F32)
    t1v = k.alloc("t1v", [2562], F32)
    sg = k.alloc("sg", [NTOK], F32)
    ob = [k.alloc("uob%d" % i, [NTOK], BF16) for i in range(2)]
    NCH = 22
    bcnt = 0

    def load(ci):
        k.dma(wg[ci % 2], wv[:, :, ci * 256:(ci + 1) * 256], eng="pool")
        k.dma(wvl[ci % 2], wv[:, :, DFF + ci * 256:DFF + (ci + 1) * 256], eng="pool")

    load(0)
    for ci in range(NCH):
        if ci + 1 < NCH:
            load(ci + 1)
        for mi in range(2):
            i = ci * 2 + mi
            res = []
            for which, w, t1 in ((0, wg[ci % 2], t1g), (1, wvl[ci % 2], t1v)):
                banks5 = []
                for tb in range(5):
                    bank = k.banks[bcnt % 8]
                    bcnt += 1
                    for kt in range(16):
                        k.mm(bank, w[:, kt, mi * 128:(mi + 1) * 128], V(uT.ap[:, kt, tb * 512:(tb + 1) * 512], uTb[tb].res),
                             start=(kt == 0), stop=(kt == 15))
                    banks5.append(bank)
                j = i if which == 0 else 44 + i
                pc = pp[:, PP_FFN + 4 * j:PP_FFN + 4 * j + 4]
                conv_epilogue(k, banks5, cb[which], t1, pc)
                res.append(pc)
            o = ob[i % 2]
            for (c0, n, t0) in T1_SEGS:
                k.act(sg[:, t0:t0 + n], t1g[:, c0:c0 + n], AF.Silu, bias=res[0][:, 3:4])
                k.stt(o[:, t0:t0 + n], t1v[:, c0:c0 + n], res[1][:, 3:4], sg[:, t0:t0 + n], ALU.add, ALU.mult)
            k.dma(AT[i * 128:(i + 1) * 128, :], o)
    k.release(m)


def phase_down(k, D, C, l):
    AT, F2 = D["AT%d" % l], D["F2_%d" % l]
    wv = D["w_down"][l].r("(k p) n -> p k n", p=128)
    KT = 44
    for half in range(2):
        m = k.mark()
        h0 = half * 1280
        aT = k.alloc("aT", [KT, 1280], BF16)
        aTk = [aT.part((slice(None), kt, slice(None)), "aTk%d_%d_%d" % (kt, half, l)) for kt in range(KT)]
        for kt in range(KT):
            k.dma(aTk[kt], AT[kt * 128:(kt + 1) * 128, h0:h0 + 1280])
        wd = [k.alloc("wd%d" % i, [KT, 256], BF16) for i in range(2)]
        stg = [k.alloc("dstg%d" % i, [256], F32) for i in range(2)]
        k.dma(wd[0], wv[:, :, 0:256], eng="pool")
        cnt = 0
        for cc in range(8):
            if cc + 1 < 8:
                k.dma(wd[(cc + 1) % 2], wv[:, :, (cc + 1) * 256:(cc + 2) * 256], eng="pool")
            w = wd[cc % 2]
            for tt in range(10):
                bank = k.banks[cnt % 8]
                for kt in range(KT):
                    k.mm(bank[:, 0:256], aTk[kt][:, tt * 128:(tt + 1) * 128], w[:, kt, :], start=(kt == 0), stop=(kt == KT - 1))
                s_ = stg[cnt % 2]
                cnt += 1
                k.copy(s_, bank[:, 0:256], eng=("act" if cnt % 2 else "dve"))
                k.dma(F2[h0 + tt * 128:h0 + (tt + 1) * 128, cc * 256:(cc + 1) * 256], s_)
        k.release(m)


def phase_ln2(k, D, C, l, Xdst):
    m = k.mark()
    F2, X1 = D["F2_%d" % l], D["X1_%d" % l]
    fb = [k.alloc("lfb%d" % i, [DM], F32) for i in range(2)]
    xb = [k.alloc("lxb%d" % i, [DM], F32) for i in range(2)]
    junk = k.alloc("ljunk", [DM], BF16)
    g2 = k.alloc("g2", [DM], F32)
    lng = k.alloc("lng2", [DM], F32)
    lnb = k.alloc("lnb2", [DM], F32)
    st = [k.alloc("lst%d" % i, [4], F32) for i in range(2)]
    k.dma(lng, D["ln2_g"][l, :].pb(128))
    k.dma(lnb, D["ln2_b"][l, :].pb(128))
    for tt in range(NTT):
        if tt == 0 or tt == 16:
            k.dma(g2, D["modv"][l, 0 if tt == 0 else 1, MOD_G2:MOD_G2 + DM].pb(128))
        f = fb[tt % 2]
        x = xb[tt % 2]
        k.dma(f, F2[tt * 128:(tt + 1) * 128, :])
        k.dma(x, X1[tt * 128:(tt + 1) * 128, :])
        k.tt(f, f, g2, ALU.mult, eng="pool")
        k.stt(f, x, ALPHA, f, ALU.mult, ALU.add)
        ln_tile(k, f, lng, lnb, x, st[tt % 2], junk)
        k.dma(Xdst[tt * 128:(tt + 1) * 128, :], x)
    k.release(m)
def build(upto="all", dbg=()):
    k = KB()
    D = declare(k, dbg)
    declare_hyena(k, D, dbg)
    C = load_consts(k, D)
    phase_mod(k, D, C)
    if upto == "mod":
        return k, k.finalize()
    for l in range(2):
        for L in (2048, 256):
            hyena_filter(k, D, C, l, L)
    if upto == "hyf":
        return k, k.finalize()
    for l in range(2):
        Xsrc = D["X0"] if l == 0 else D["X_1"]
        m0 = k.mark()
        uT = k.alloc("uT", [16, NTOK], BF16)
        uTb = [uT.part((slice(None), slice(None), slice(tb * 512, (tb + 1) * 512)), "uTb%d_%d" % (tb, l)) for tb in range(5)]
        front(k, D, C, Xsrc, l, MOD_SC1, MOD_SH1, uT, uTb)
        phase_in(k, D, C, l, uT, uTb)
        k.release(m0)
        if upto == "in":
            return k, k.finalize()
        only = upto if upto in ("hy", "ssd", "gdn", "ret") else None
        if only in (None, "hy"):
            for si in range(3):
                hyena_conv(k, D, C, l, si)
        if only is None:
            for kind in ("ssd", "gdn", "ret"):
                la_mixer(k, D, C, l, kind, 0)
            la_mixer_pair(k, D, C, l, (("ssd", 1), ("gdn", 1)))
            la_mixer_pair(k, D, C, l, (("ssd", 2), ("gdn", 2)))
            la_mixer_pair(k, D, C, l, (("ret", 1), ("ret", 2)))
        else:
            for kind in ("ssd", "gdn", "ret"):
                if only == kind:
                    for si in range(3):
                        la_mixer(k, D, C, l, kind, si)
        if only is not None or upto == "mix":
            return k, k.finalize()
        phase_out(k, D, C, l, Xsrc)
        if upto == "out":
            return k, k.finalize()
        m0 = k.mark()
        uT = k.alloc("uT", [16, NTOK], BF16)
        uTb = [uT.part((slice(None), slice(None), slice(tb * 512, (tb + 1) * 512)), "uTc%d_%d" % (tb, l)) for tb in range(5)]
        front(k, D, C, D["X1_%d" % l], l, MOD_SC2, MOD_SH2, uT, uTb)
        phase_up(k, D, C, l, uT, uTb)
        k.release(m0)
        if upto == "up":
            return k, k.finalize()
        phase_down(k, D, C, l)
        phase_ln2(k, D, C, l, D["X_1"] if l == 0 else D["X_2"])
        if upto == "l0":
            return k, k.finalize()
    return k, k.finalize()


def shared_inputs(inp):
    m = host_shared(inp)
    for L in (2048, 256):
        m.update(hyena_consts(L))
    return m


_CACHE = {}


def kernel(**inputs):
    inp = {k_: np.asarray(v) for k_, v in inputs.items()}
    if "nc" not in _CACHE:
        _CACHE["nc"] = build("all", ())[1]
    nc = _CACHE["nc"]
    shared = shared_inputs(inp)
    in_maps = []
    for core in range(8):
        m = dict(shared)
        m.update(host_prep(inp, core))
        in_maps.append(m)
    res = run_bass_kernel_spmd(nc, in_maps, core_ids=list(range(8)))
    y_prompt = np.zeros((16, 256, DM), np.float32)
    y_sample = np.zeros((8, 2048, DM), np.float32)
    ns_ssd = np.zeros((16, 2, 2, 8, 64, 128), np.float32)
    ns_gdn = np.zeros((16, 2, 2, 4, 128, 128), np.float32)
    ns_ret = np.zeros((16, 2, 2, 4, 128, 128), np.float32)
    for core in range(8):
        r = res.results[core]
        x2 = np.asarray(r["X_2"])
        y_sample[core] = x2[0:2048]
        y_prompt[2 * core] = x2[2048:2304]
        y_prompt[2 * core + 1] = x2[2304:2560]
        for pi in range(2):
            ns_ssd[2 * core + pi] = np.asarray(r["ns_ssd"])[pi]
            ns_gdn[2 * core + pi] = np.asarray(r["ns_gdn"])[pi]
            ns_ret[2 * core + pi] = np.asarray(r["ns_ret"])[pi]
    return (y_prompt, y_sample, ns_ssd, ns_gdn, ns_ret)
```

```python
import math
import numpy as np
import ml_dtypes
from contextlib import ExitStack
import concourse.bass as bass
import concourse.mybir as mybir
from concourse.bass_utils import run_bass_kernel_spmd

F32 = mybir.dt.float32
BF16 = mybir.dt.bfloat16
AF = mybir.ActivationFunctionType
ALU = mybir.AluOpType
AX = mybir.AxisListType
DSIZE = {F32: 4, BF16: 2}
ENGS = ("pe", "dve", "act", "pool", "sp")
MULTIW = True


class Res:
    __slots__ = ("name", "last_w", "readers", "psum")

    def __init__(self, name, psum=False):
        self.name = name
        self.last_w = None
        self.readers = []
        self.psum = psum


class V:
    def __init__(self, ap, res):
        self.ap = ap
        self.res = res

    def __getitem__(self, idx):
        return V(self.ap[idx], self.res)

    def r(self, pattern, **kw):
        return V(self.ap.rearrange(pattern, **kw), self.res)

    def bc(self, dt):
        return V(self.ap.bitcast(dt), self.res)

    def pb(self, n):
        return V(self.ap.partition_broadcast(n), self.res)

    def tb(self, shape):
        return V(self.ap.to_broadcast(list(shape)), self.res)

    def us(self, axis):
        return V(self.ap.unsqueeze(axis), self.res)

    def part(self, idx, tag):
        return V(self.ap[idx], [Res(tag)])

    def newres(self, tag):
        return V(self.ap, [Res(tag)])

    @property
    def shape(self):
        return self.ap.shape


class Op:
    __slots__ = ("eng", "fn", "reads", "writes", "is_dma", "idx", "deps", "sem", "val", "needs_inc", "barrier")


class KB:
    def __init__(self, n_dma_sems=48, arena_bytes=211968):
        self.nc = bass.Bass("TRN2", target_bir_lowering=False)
        self.es = ExitStack()
        self.ops = []
        self.n_dma_sems = n_dma_sems
        self.arena_bytes = arena_bytes
        self.arena = self.es.enter_context(self.nc.sbuf_tensor("arena", [128, arena_bytes // 4], F32))[:]
        self.top = 0
        self.peak = 0
        self.banks = []
        for i in range(8):
            t = self.es.enter_context(self.nc.psum_tensor("bank%d" % i, [128, 512], F32))
            self.banks.append(V(t[:], [Res("bank%d" % i, psum=True)]))
        self.nres = 0

    def alloc(self, name, free_shape, dtype, parts=128):
        n = int(np.prod(free_shape)) * DSIZE[dtype]
        n = (n + 63) // 64 * 64
        off = self.top
        self.top += n
        self.peak = max(self.peak, self.top)
        assert self.top <= self.arena_bytes, "SBUF arena overflow at %s: %d" % (name, self.top)
        ap = self.arena[:, off // 4:(off + n) // 4]
        if dtype != F32:
            ap = ap.bitcast(dtype)
        nel = int(np.prod(free_shape))
        ap = ap[:, 0:nel]
        if len(free_shape) == 2:
            ap = ap.rearrange("p (a b) -> p a b", a=free_shape[0])
        elif len(free_shape) == 3:
            ap = ap.rearrange("p (a b c) -> p a b c", a=free_shape[0], b=free_shape[1])
        if parts != 128:
            ap = ap[0:parts]
        self.nres += 1
        return V(ap, [Res("%s#%d" % (name, self.nres))])

    def mark(self):
        return self.top

    def release(self, m):
        self.top = m
        self.barrier()

    def dram(self, name, shape, dtype, kind="Internal"):
        t = self.nc.dram_tensor(name, list(shape), dtype, kind=kind)
        return V(t.ap(), [Res(name)])

    def op(self, eng, fn, reads, writes, is_dma=False):
        o = Op()
        o.eng = eng
        o.fn = fn
        rr = []
        for v in reads:
            rr.extend(v.res)
        ww = []
        for v in writes:
            ww.extend(v.res)
        o.reads = rr
        o.writes = ww
        o.is_dma = is_dma
        o.idx = len(self.ops)
        o.needs_inc = is_dma
        o.barrier = False
        self.ops.append(o)
        return o

    def barrier(self):
        o = self.op("sp", None, [], [])
        o.barrier = True

    def dma(self, out, in_, eng="sp", **kw):
        return self.op(eng, lambda e: e.dma_start(out=out.ap, in_=in_.ap, **kw), [in_], [out], is_dma=True)

    def mm(self, out, lhsT, rhs, start=True, stop=True, **kw):
        return self.op("pe", lambda e: e.matmul(out.ap, lhsT.ap, rhs.ap, start=start, stop=stop, **kw),
                       [lhsT, rhs], [out])

    def tr(self, out, in_, ident):
        return self.op("pe", lambda e: e.transpose(out.ap, in_.ap, ident.ap), [in_, ident], [out])

    def act(self, out, in_, func, bias=None, scale=None, accum=None):
        reads = [in_]
        kw = {}
        if bias is not None:
            if isinstance(bias, V):
                reads.append(bias)
                kw["bias"] = bias.ap
            else:
                kw["bias"] = float(bias)
        if scale is not None:
            if isinstance(scale, V):
                reads.append(scale)
                kw["scale"] = scale.ap
            else:
                kw["scale"] = float(scale)
        writes = [out]
        if accum is not None:
            kw["accum_out"] = accum.ap
            writes.append(accum)
        return self.op("act", lambda e: e.activation(out.ap, in_.ap, func, **kw), reads, writes)

    def tt(self, out, a, b, op, eng="dve"):
        return self.op(eng, lambda e: e.tensor_tensor(out.ap, a.ap, b.ap, op), [a, b], [out])

    def ts(self, out, a, s1, op0, s2=None, op1=None, eng="dve", accum=None):
        reads = [a]

        def cv(s):
            if isinstance(s, V):
                reads.append(s)
                return s.ap
            return None if s is None else float(s)

        c1 = cv(s1)
        c2 = cv(s2)
        kw = {}
        writes = [out]
        if accum is not None:
            kw["accum_out"] = accum.ap
            writes.append(accum)
        if op1 is None:
            return self.op(eng, lambda e: e.tensor_scalar(out.ap, a.ap, c1, None, op0, **kw), reads, writes)
        return self.op(eng, lambda e: e.tensor_scalar(out.ap, a.ap, c1, c2, op0, op1, **kw), reads, writes)

    def stt(self, out, a, s, b, op0, op1, eng="dve"):
        reads = [a, b]
        if isinstance(s, V):
            reads.append(s)
            sv = s.ap
        else:
            sv = float(s)
        return self.op(eng, lambda e: e.scalar_tensor_tensor(out.ap, a.ap, sv, b.ap, op0, op1), reads, [out])

    def copy(self, out, in_, eng="dve"):
        if eng == "act":
            return self.op("act", lambda e: e.copy(out.ap, in_.ap), [in_], [out])
        return self.op(eng, lambda e: e.tensor_copy(out.ap, in_.ap), [in_], [out])

    def memset(self, out, val, eng="dve"):
        return self.op(eng, lambda e: e.memset(out.ap, val), [], [out])

    def recip(self, out, in_):
        return self.op("dve", lambda e: e.reciprocal(out.ap, in_.ap), [in_], [out])

    def scan(self, out, d0, d1, init, op0, op1):
        return self.op("dve", lambda e: e.tensor_tensor_scan(out.ap, d0.ap, d1.ap, init, op0, op1), [d0, d1], [out])

    def reduce(self, out, in_, op, axis=AX.X, eng="dve"):
        return self.op(eng, lambda e: e.tensor_reduce(out.ap, in_.ap, axis, op), [in_], [out])

    def _compress(self, idxs):
        ops = self.ops
        best = {}
        out = []
        for d in idxs:
            po = ops[d]
            if po.is_dma:
                out.append(d)
            elif po.eng not in best or best[po.eng] < d:
                best[po.eng] = d
        out.extend(best.values())
        return out

    def finalize(self):
        ops = self.ops
        nc = self.nc
        pending = {e: set() for e in ENGS}
        last_on = {e: None for e in ENGS}
        dmas_since = []
        for o in ops:
            if o.barrier:
                B = set(dmas_since)
                for e in ENGS:
                    if last_on[e] is not None:
                        B.add(last_on[e])
                for e in ENGS:
                    pending[e] |= B
                dmas_since = []
                o.deps = []
                continue
            deps = set()
            for r in o.reads:
                if r.last_w is not None:
                    deps.update(r.last_w)
                if r.psum:
                    for rd in r.readers:
                        if ops[rd].eng != o.eng:
                            deps.add(rd)
            for w in o.writes:
                if w.last_w is not None:
                    if not (MULTIW and o.is_dma and not w.readers and all(ops[x].is_dma for x in w.last_w)):
                        deps.update(w.last_w)
                deps.update(w.readers)
            for r in o.reads:
                r.readers.append(o.idx)
                if len(r.readers) > 64:
                    r.readers = self._compress(r.readers)
            for w in o.writes:
                if MULTIW and o.is_dma and w.last_w is not None and not w.readers and all(ops[x].is_dma for x in w.last_w):
                    w.last_w = w.last_w + [o.idx]
                else:
                    w.last_w = [o.idx]
                w.readers = []
            if pending[o.eng]:
                deps |= pending[o.eng]
                pending[o.eng] = set()
            deps.discard(o.idx)
            deps = self._compress(deps)
            o.deps = [d for d in deps
                      if not (o.eng == "pe" and ops[d].eng == "pe" and not ops[d].is_dma and not o.is_dma)]
            for d in o.deps:
                ops[d].needs_inc = True
            if o.is_dma:
                dmas_since.append(o.idx)
            else:
                last_on[o.eng] = o.idx
        with ExitStack() as es2:
            esem = {e: es2.enter_context(nc.semaphore("sem_" + e)) for e in ENGS}
            NSW = 24
            NT = self.n_dma_sems + NSW
            dsem = [es2.enter_context(nc.semaphore("dsem%d" % i)) for i in range(NT)]
            ecount = {e: 0 for e in ENGS}
            dcount = [0] * NT
            dlast = [None] * NT
            nd_hw = 0
            nd_sw = 0
            for o in ops:
                if o.barrier:
                    continue
                if o.is_dma:
                    if o.eng == "pool":
                        s = self.n_dma_sems + nd_sw % NSW
                        nd_sw += 1
                    else:
                        s = nd_hw % self.n_dma_sems
                        nd_hw += 1
                    if dlast[s] is not None:
                        o.deps.append(dlast[s])
                    dcount[s] += 16
                    o.sem = dsem[s]
                    o.val = dcount[s]
                    dlast[s] = o.idx
                elif o.needs_inc:
                    ecount[o.eng] += 1
                    o.sem = esem[o.eng]
                    o.val = ecount[o.eng]
            per_eng = {e: [] for e in ENGS}
            for o in ops:
                if not o.barrier:
                    per_eng[o.eng].append(o)
            self.stats = {e: len(per_eng[e]) for e in ENGS}
            self.stats["incs"] = dict(ecount)
            nwaits = [0]

            def emit(eng_name, e):
                seen = {}
                for o in per_eng[eng_name]:
                    need = {}
                    for d in o.deps:
                        po = ops[d]
                        key = id(po.sem)
                        if key not in need or need[key][1] < po.val:
                            need[key] = (po.sem, po.val)
                    for key, (sem, val) in need.items():
                        if seen.get(key, 0) >= val:
                            continue
                        e.wait_ge(sem, val)
                        nwaits[0] += 1
                        seen[key] = val
                    ins = o.fn(e)
                    if o.needs_inc:
                        ins.then_inc(o.sem, 16 if o.is_dma else 1)
                if eng_name == "sp":
                    for s in range(NT):
                        if dcount[s] > 0 and seen.get(id(dsem[s]), 0) < dcount[s]:
                            e.wait_ge(dsem[s], dcount[s])
                    for en in ENGS:
                        if en != "sp" and ecount[en] > 0:
                            e.wait_ge(esem[en], ecount[en])

            with nc.Block() as block:
                @block.sync
                def _(e):
                    emit("sp", e)

                @block.tensor
                def _(e):
                    emit("pe", e)

                @block.vector
                def _(e):
                    emit("dve", e)

                @block.scalar
                def _(e):
                    emit("act", e)

                @block.gpsimd
                def _(e):
                    emit("pool", e)
            self.stats["waits"] = nwaits[0]
            self.stats["sbuf_peak"] = self.peak
        self.es.close()
        return nc
DM = 2048
NTOK = 2560
NTT = 20
SEQS = ((0, 2048), (2048, 256), (2304, 256))
DFF = 5632
ALPHA = 4.0 ** 0.25
N_IN = 7200

def _tiles():
    T = []
    def add(name, col, n, conv, func, l2, fm, tm):
        for i in range(n):
            T.append((name, col + 128 * i, conv, func, l2, None if fm is None else fm + 128 * i,
                      None if tm is None else tm + 128 * i))
    add("hy_v", 0, 4, True, "id", None, None, 0)
    add("hy_x1", 512, 4, True, "id", None, None, 512)
    add("hy_x2", 1024, 4, True, "id", None, 0, None)
    add("s_x", 1536, 4, True, "silu", None, 512, 1024)
    add("s_b", 2048, 2, True, "silu", None, 1024, 1536)
    add("s_c", 2304, 2, True, "silu", None, 1280, None)
    add("g_q", 2560, 4, True, "silu", 128.0 ** -0.5, 1536, None)
    add("g_k", 3072, 4, True, "silu", 1.0, 2048, 1792)
    add("g_v", 3584, 4, True, "silu", None, None, 2304)
    add("s_z", 4096, 4, False, "silu", None, 2560, None)
    add("g_z", 4624, 4, False, "silu", None, 3072, None)
    add("r_g", 6688, 4, False, "silu", None, 3584, None)
    return T
FM_TILES = _tiles()
NFM = len(FM_TILES)
FM_ROWS = 4096
TM_COLS = 4352
TM_RQ, TM_RK, TM_RV = 2816, 3328, 3840
def _perm():
    p = []
    for t in FM_TILES:
        p.extend(range(t[1], t[1] + 128))
    p.extend(range(5152, 5152 + 1536))
    p.extend(range(4608, 4624))
    p.extend(range(5136, 5152))
    return np.array(p)
W_IN_PERM = _perm()
PP_CONV = 0
PP_FFN = 176
PP_SSDNW = 528
PP_GDNNW = 532
PP_HYB1, PP_HYF0, PP_HYB2, PP_HYF1 = 533, 534, 535, 536
PP_SSD_DTB = 537
PP_SSD_ALOG = 553
PP_SSD_D = 569
PP_GDN_DTB = 585
PP_GDN_ALOG = 593
PP_RET_DEC = 601
PP_SSD_DCOL = 612
NPP = 620
MOD_SH1, MOD_SC1, MOD_G1, MOD_SH2, MOD_SC2, MOD_G2 = [i * 2048 for i in range(6)]


def host_prep(inp, core):
    f32 = np.float32
    bf = ml_dtypes.bfloat16
    m = {}
    m["X0"] = np.ascontiguousarray(np.concatenate(
        [inp["x_sample"][core], inp["x_prompt"][2 * core], inp["x_prompt"][2 * core + 1]], 0))
    m["cond"] = np.ascontiguousarray(np.stack([inp["c"][core], inp["c_ctx"]], 0))
    m["st_ssd"] = np.ascontiguousarray(inp["state_ssd"][core])
    m["st_gdn"] = np.ascontiguousarray(inp["state_gdn"][core])
    m["st_ret"] = np.ascontiguousarray(inp["state_ret"][core])
    return m


def host_shared(inp):
    f32 = np.float32
    bf = ml_dtypes.bfloat16
    m = {}
    m["w_mod"] = inp["w_mod"]
    m["b_mod"] = inp["b_mod"]
    m["w_in"] = np.ascontiguousarray(inp["w_in"][:, :, W_IN_PERM])
    m["w_out"] = inp["w_out"]
    m["w_up"] = inp["w_up"]
    m["w_down"] = inp["w_down"]
    m["hy_w1"] = inp["hy_w1"]
    m["hy_w2"] = inp["hy_w2"]
    m["hy_w3"] = inp["hy_w3"]
    pp = np.zeros((2, 128, NPP), f32)
    for l in range(2):
        for i, t in enumerate(FM_TILES):
            if t[2]:
                ch = slice(t[1], t[1] + 128)
                pp[l, :, PP_CONV + 4 * i:PP_CONV + 4 * i + 3] = inp["conv_w"][l][:, ch].T
                pp[l, :, PP_CONV + 4 * i + 3] = inp["conv_b"][l][ch]
        for j in range(88):
            ch = slice(128 * j, 128 * j + 128)
            pp[l, :, PP_FFN + 4 * j:PP_FFN + 4 * j + 3] = inp["ffn_conv_w"][l][:, ch].T
            pp[l, :, PP_FFN + 4 * j + 3] = inp["ffn_conv_b"][l][ch]
        pp[l, :, PP_SSDNW:PP_SSDNW + 4] = inp["ssd_norm_w"][l].reshape(4, 128).T
        pp[l, :, PP_GDNNW] = inp["gdn_norm_w"][l]
        pp[l, :64, PP_HYB1] = inp["hy_b1"][l]
        pp[l, :64, PP_HYF0] = inp["hy_freq"][l, 0]
        pp[l, :64, PP_HYB2] = inp["hy_b2"][l]
        pp[l, :64, PP_HYF1] = inp["hy_freq"][l, 1]
        pp[l, :, PP_SSD_DTB:PP_SSD_DTB + 16] = inp["ssd_dt_bias"][l].reshape(1, 16)
        pp[l, :, PP_SSD_ALOG:PP_SSD_ALOG + 16] = inp["ssd_A_log"][l].reshape(1, 16)
        pp[l, :, PP_SSD_D:PP_SSD_D + 16] = inp["ssd_D"][l].reshape(1, 16)
        pp[l, :, PP_GDN_DTB:PP_GDN_DTB + 8] = inp["gdn_dt_bias"][l].reshape(1, 8)
        pp[l, :, PP_GDN_ALOG:PP_GDN_ALOG + 8] = inp["gdn_A_log"][l].reshape(1, 8)
        pp[l, :, PP_RET_DEC:PP_RET_DEC + 8] = inp["ret_decay"][l].reshape(1, 8)
        for d in range(2):
            for t in range(4):
                pp[l, 0:64, PP_SSD_DCOL + 4 * d + t] = inp["ssd_D"][l, d, 2 * t]
                pp[l, 64:128, PP_SSD_DCOL + 4 * d + t] = inp["ssd_D"][l, d, 2 * t + 1]
    m["pp"] = pp
    m["ln1_g"] = inp["ln1_g"]; m["ln1_b"] = inp["ln1_b"]; m["ln2_g"] = inp["ln2_g"]; m["ln2_b"] = inp["ln2_b"]
    m["hy_bias"] = np.ascontiguousarray(inp["hy_bias"].reshape(2, 1024))
    m["identb"] = np.eye(128).astype(bf)
    m["identf"] = np.eye(128).astype(f32)
    m["onesb"] = np.ones((128, 128)).astype(bf)
    m["onesf"] = np.ones((128, 128)).astype(f32)
    jj = np.arange(128)[:, None]
    ii = np.arange(128)[None, :]
    m["maskF"] = (jj <= ii).astype(f32)
    m["maskB"] = (jj >= ii).astype(f32)
    m["nmaskSF"] = -(jj < ii).astype(f32)
    m["nmaskSB"] = -(jj > ii).astype(f32)
    blk = np.zeros((14, 128, 128), f32)
    for lev in range(7):
        sz = 1 << lev
        mk = ((jj // (2 * sz)) == (ii // (2 * sz))) & ((jj // sz) % 2 == 1) & ((ii // sz) % 2 == 0)
        blk[lev] = mk
        blk[7 + lev] = mk.T
    m["blkM"] = blk.astype(bf)
    r = np.arange(2048) // 64
    col = np.arange(2048) % 64
    inv = np.power(10000.0, -np.arange(32, dtype=np.float64) / 32)
    ang = np.concatenate([r[:, None] * inv, col[:, None] * inv], -1)
    m["rope_cos"] = np.cos(ang).astype(f32)
    m["rope_sin"] = np.sin(ang).astype(f32)
    return m
FUNC = {"id": AF.Identity, "silu": AF.Silu}


def declare(k, dbg):
    D = {}
    def inp(name, shape, dt=F32):
        D[name] = k.dram(name, shape, dt, "ExternalInput")
    def scr(name, shape, dt, out=False):
        D[name] = k.dram(name, shape, dt, "ExternalOutput" if (out or name in dbg) else "Internal")
    inp("X0", [NTOK, DM]); inp("cond", [2, DM])
    inp("st_ssd", [2, 2, 8, 64, 128]); inp("st_gdn", [2, 2, 4, 128, 128]); inp("st_ret", [2, 2, 4, 128, 128])
    inp("w_mod", [2, DM, 6 * DM]); inp("b_mod", [2, 6 * DM]); inp("w_in", [2, DM, N_IN]); inp("w_out", [2, DM, DM])
    inp("w_up", [2, DM, 2 * DFF]); inp("w_down", [2, DFF, DM])
    inp("hy_w1", [2, 17, 64]); inp("hy_w2", [2, 64, 64]); inp("hy_w3", [2, 64, 2048])
    inp("pp", [2, 128, NPP])
    for n in ("ln1_g", "ln1_b", "ln2_g", "ln2_b"):
        inp(n, [2, DM])
    inp("hy_bias", [2, 1024])
    inp("identb", [128, 128], BF16); inp("identf", [128, 128]); inp("onesb", [128, 128], BF16); inp("onesf", [128, 128])
    inp("rope_cos", [2048, 64]); inp("rope_sin", [2048, 64])
    for n in ("maskF", "maskB", "nmaskSF", "nmaskSB"):
        inp(n, [128, 128])
    inp("blkM", [14, 128, 128], BF16)
    scr("modv", [2, 2, 6 * DM], F32)
    for l in range(2):
        scr("FM%d" % l, [FM_ROWS, NTOK], BF16)
        scr("TM%d" % l, [NTOK, TM_COLS], BF16)
        scr("SM%d" % l, [NTOK, 32], F32)
        scr("YT%d" % l, [DM, NTOK], BF16)
        scr("X1_%d" % l, [NTOK, DM], F32)
        scr("AT%d" % l, [DFF, NTOK], BF16)
        scr("F2_%d" % l, [NTOK, DM], F32)
    scr("X_1", [NTOK, DM], F32)
    scr("X_2", [NTOK, DM], F32, out=True)
    scr("ns_ssd", [2, 2, 2, 8, 64, 128], F32, out=True)
    scr("ns_gdn", [2, 2, 2, 4, 128, 128], F32, out=True)
    scr("ns_ret", [2, 2, 2, 4, 128, 128], F32, out=True)
    return D


def load_consts(k, D):
    C = {}
    for n, dt in (("identb", BF16), ("identf", F32), ("onesb", BF16), ("onesf", F32)):
        C[n] = k.alloc(n, [128], dt)
        k.dma(C[n], D[n])
    C["pp"] = [k.alloc("pp%d" % l, [NPP], F32) for l in range(2)]
    for l in range(2):
        k.dma(C["pp"][l], D["pp"][l])
    return C


def phase_mod(k, D, C):
    m = k.mark()
    cs = k.alloc("cs", [DM], F32, parts=2)
    k.dma(cs, D["cond"])
    k.act(cs, cs, AF.Silu)
    scT = k.alloc("scT", [16, 2], BF16)
    pt = k.banks[7]
    for kt in range(16):
        k.tr(pt[:, 2 * kt:2 * kt + 2], cs[:, kt * 128:(kt + 1) * 128], C["identf"][0:2, 0:2])
    k.copy(scT, pt[:, 0:32].r("p (a b) -> p a b", a=16))
    wb = [k.alloc("wmod%d" % i, [16, 512], BF16) for i in range(2)]
    bm = k.alloc("bmod", [6 * DM], F32, parts=2)
    res = k.alloc("modres", [6 * DM], F32, parts=2)
    cnt = 0
    for l in range(2):
        k.dma(bm, D["b_mod"][l, :].pb(2))
        wv = D["w_mod"][l].r("(k p) n -> p k n", p=128)
        for c in range(24):
            w = wb[cnt % 2]
            k.dma(w, wv[:, :, c * 512:(c + 1) * 512], eng="pool")
            bank = k.banks[cnt % 4]
            for kt in range(16):
                k.mm(bank[0:2, :], scT[:, kt, :], w[:, kt, :], start=(kt == 0), stop=(kt == 15))
            k.tt(res[:, c * 512:(c + 1) * 512], bank[0:2, :], bm[:, c * 512:(c + 1) * 512], ALU.add)
            cnt += 1
        k.dma(D["modv"][l], res)
    k.release(m)


def front(k, D, C, Xsrc, l, off_sc, off_sh, uT, uTb):
    m = k.mark()
    scb = [k.alloc("scb%d" % i, [DM], F32) for i in range(2)]
    shb = [k.alloc("shb%d" % i, [DM], F32) for i in range(2)]
    for ci in range(2):
        k.dma(scb[ci], D["modv"][l, ci, off_sc:off_sc + DM].pb(128))
        k.dma(shb[ci], D["modv"][l, ci, off_sh:off_sh + DM].pb(128))
        k.ts(scb[ci], scb[ci], 1.0, ALU.add)
    xb = [k.alloc("xb%d" % i, [DM], F32) for i in range(2)]
    ub = [k.alloc("ub%d" % i, [DM], BF16) for i in range(2)]
    for tt in range(NTT):
        ci = 0 if tt < 16 else 1
        x = xb[tt % 2]
        u = ub[tt % 2]
        k.dma(x, Xsrc[tt * 128:(tt + 1) * 128, :])
        k.tt(x, x, scb[ci], ALU.mult)
        k.tt(u, x, shb[ci], ALU.add, eng="pool")
        for half in range(2):
            bank = k.banks[6 + half].bc(BF16)
            for j in range(8):
                kt = half * 8 + j
                k.tr(bank[:, j * 128:(j + 1) * 128], u[:, kt * 128:(kt + 1) * 128], C["identb"])
            dst = V(uT.ap[:, half * 8:(half + 1) * 8, tt * 128:(tt + 1) * 128], uTb[tt // 4].res)
            k.copy(dst, bank.r("p (a b) -> p a b", a=8), eng=("act" if half == 0 else "dve"))
    k.release(m)


def seq_cols(tb):
    if tb < 4:
        return [(0, 512, 1 + 512 * tb, 512 * tb)]
    return [(0, 256, 2050, 2048), (256, 256, 2307, 2304)]


def conv_epilogue(k, banks5, cb, t1, wcol, eng2="dve"):
    for tb in range(5):
        for (pc, n, bc0, _t0) in seq_cols(tb):
            k.copy(cb[:, bc0:bc0 + n], banks5[tb][:, pc:pc + n], eng="act")
    k.ts(t1, cb[:, 0:2562], wcol[:, 0:1], ALU.mult)
    k.stt(t1, cb[:, 1:2563], wcol[:, 1:2], t1, ALU.mult, ALU.add)
    k.stt(t1, cb[:, 2:2564], wcol[:, 2:3], t1, ALU.mult, ALU.add, eng=eng2)


T1_SEGS = ((0, 2048, 0), (2049, 256, 2048), (2306, 256, 2304))


def phase_in(k, D, C, l, uT, uTb):
    m = k.mark()
    pp = C["pp"][l]
    FM, TM, SM = D["FM%d" % l], D["TM%d" % l], D["SM%d" % l]
    wv = D["w_in"][l].r("(k p) n -> p k n", p=128)
    wb = [k.alloc("win%d" % i, [16, 512], BF16) for i in range(2)]
    cb = [k.alloc("cb%d" % i, [2564], F32) for i in range(2)]
    for c in cb:
        for g in (0, 2049, 2306, 2563):
            k.memset(c[:, g:g + 1], 0.0)
    t1 = k.alloc("t1", [2562], F32)
    ob = [k.alloc("ob%d" % i, [NTOK], BF16) for i in range(2)]
    sf = k.alloc("sf", [NTOK], F32)
    sq = k.alloc("sq", [NTOK], BF16)
    tms = [k.alloc("tms%d" % i, [NTT, 128], BF16) for i in range(2)]
    nchunks = NFM // 4
    bcnt = 0
    ntm = 0

    def load_chunk(ci, ncols=512):
        k.dma(wb[ci % 2][:, :, 0:ncols], wv[:, :, ci * 512:ci * 512 + ncols], eng="pool")

    load_chunk(0)
    for ci in range(nchunks):
        if ci + 1 < nchunks + 3:
            load_chunk(ci + 1)
        w = wb[ci % 2]
        for mi in range(4):
            ti = ci * 4 + mi
            name, ocol, is_conv, func, l2, fm_row, tm_col = FM_TILES[ti]
            banks5 = []
            for tb in range(5):
                bank = k.banks[bcnt % 6]
                bcnt += 1
                for kt in range(16):
                    k.mm(bank, w[:, kt, mi * 128:(mi + 1) * 128], V(uT.ap[:, kt, tb * 512:(tb + 1) * 512], uTb[tb].res),
                         start=(kt == 0), stop=(kt == 15))
                banks5.append(bank)
            o = ob[ti % 2]
            pc = pp[:, PP_CONV + 4 * ti:PP_CONV + 4 * ti + 4]
            if is_conv:
                conv_epilogue(k, banks5, cb[ti % 2], t1, pc)
                dst = sf if l2 is not None else o
                for (c0, n, t0) in T1_SEGS:
                    k.act(dst[:, t0:t0 + n], t1[:, c0:c0 + n], FUNC[func], bias=pc[:, 3:4])
                if l2 is not None:
                    k.tt(sq, sf, sf, ALU.mult, eng="pool")
                    for tb in range(5):
                        bank = k.banks[6 + tb % 2]
                        k.mm(bank, C["onesb"], sq[:, tb * 512:(tb + 1) * 512])
                        rn = t1[:, tb * 512:(tb + 1) * 512]
                        k.act(rn, bank, AF.Sqrt, bias=1e-6)
                        k.recip(rn, rn)
                        k.stt(o[:, tb * 512:(tb + 1) * 512], rn, float(l2), sf[:, tb * 512:(tb + 1) * 512],
                              ALU.mult, ALU.mult)
            else:
                for tb in range(5):
                    k.act(o[:, tb * 512:(tb + 1) * 512], banks5[tb], FUNC[func])
            if fm_row is not None:
                k.dma(FM[fm_row:fm_row + 128, :], o)
            if tm_col is not None:
                ts_ = tms[ntm % 2]
                ntm += 1
                for g in range(3):
                    n8 = 8 if g < 2 else 4
                    bank = k.banks[6 + g % 2].bc(BF16)
                    for j in range(n8):
                        tt = g * 8 + j
                        k.tr(bank[:, j * 128:(j + 1) * 128], o[:, tt * 128:(tt + 1) * 128], C["identb"])
                    k.copy(ts_[:, g * 8:g * 8 + n8, :], bank[:, 0:n8 * 128].r("p (a b) -> p a b", a=n8),
                           eng=("act" if g == 1 else "dve"))
                k.dma(TM[:, tm_col:tm_col + 128].r("(t p) c -> p t c", p=128), ts_)
    stg = [k.alloc("stg%d" % i, [512], F32) for i in range(2)]
    stb = [k.alloc("stb%d" % i, [512], BF16) for i in range(2)]
    rc = [k.alloc("rc%d" % i, [64], F32) for i in range(2)]
    rs = [k.alloc("rs%d" % i, [64], F32) for i in range(2)]
    ra = k.alloc("ra", [4, 64], F32)
    rb = k.alloc("rb", [4, 64], F32)
    sms = [k.alloc("sms%d" % i, [32], F32) for i in range(2)]
    cnt = 0
    for j, (tmc, scale, rope) in enumerate(((TM_RQ, 1.0, True), (TM_RK, 128.0 ** -0.5, True), (TM_RV, 1.0, False))):
        ci = nchunks + j
        if j < 2:
            load_chunk(ci + 1)
        else:
            load_chunk(ci + 1, 32)
        w = wb[ci % 2]
        for tt in range(NTT):
            bank = k.banks[bcnt % 6]
            bcnt += 1
            for kt in range(16):
                k.mm(bank, V(uT.ap[:, kt, tt * 128:(tt + 1) * 128], uTb[tt // 4].res), w[:, kt, :],
                     start=(kt == 0), stop=(kt == 15))
            sb_ = stb[cnt % 2]
            if rope and tt < 16:
                sg = stg[cnt % 2]
                k.act(sg, bank, AF.Copy, scale=scale)
                c_, s_ = rc[cnt % 2], rs[cnt % 2]
                k.dma(c_, D["rope_cos"][tt * 128:(tt + 1) * 128, :])
                k.dma(s_, D["rope_sin"][tt * 128:(tt + 1) * 128, :])
                xv = sg.r("p (h two d) -> p h two d", h=4, two=2)
                ov = sb_.r("p (h two d) -> p h two d", h=4, two=2)
                cbv = c_.us(1).tb([128, 4, 64])
                sbv = s_.us(1).tb([128, 4, 64])
                k.tt(ra, xv[:, :, 0, :], cbv, ALU.mult)
                k.tt(rb, xv[:, :, 1, :], sbv, ALU.mult, eng="pool")
                k.tt(ov[:, :, 0, :], ra, rb, ALU.subtract)
                k.tt(ra, xv[:, :, 0, :], sbv, ALU.mult)
                k.tt(rb, xv[:, :, 1, :], cbv, ALU.mult, eng="pool")
                k.tt(ov[:, :, 1, :], ra, rb, ALU.add)
            else:
                k.act(sb_, bank, AF.Copy, scale=scale)
            k.dma(TM[tt * 128:(tt + 1) * 128, tmc:tmc + 512], sb_)
            cnt += 1
    w = wb[(nchunks + 3) % 2]
    for tt in range(NTT):
        bank = k.banks[bcnt % 6]
        bcnt += 1
        for kt in range(16):
            k.mm(bank[:, 0:32], V(uT.ap[:, kt, tt * 128:(tt + 1) * 128], uTb[tt // 4].res), w[:, kt, 0:32],
                 start=(kt == 0), stop=(kt == 15))
        s_ = sms[tt % 2]
        k.copy(s_, bank[:, 0:32])
        k.dma(SM[tt * 128:(tt + 1) * 128, :], s_)
    k.release(m)
class PsumPool:
    def __init__(self, k):
        self.k = k
        self.wide = [k.banks[i] for i in range(4)]
        self.small = []
        for r in range(4):
            for b in range(4, 8):
                self.small.append(k.banks[b][:, r * 128:(r + 1) * 128])
        self.wi = 0
        self.si = 0

    def w(self):
        v = self.wide[self.wi % 4]
        self.wi += 1
        return v

    def s(self):
        v = self.small[self.si % 16]
        self.si += 1
        return v


def load_masks(k, D):
    M = {}
    for n in ("maskF", "maskB", "nmaskSF", "nmaskSB"):
        M[n] = k.alloc(n, [128], F32)
        k.dma(M[n], D[n])
    M["blk"] = k.alloc("blk", [14, 128], BF16)
    k.dma(M["blk"], D["blkM"].r("a p c -> p a c"))
    return M


def la_mixer_gen(k, D, C, l, kind, si, own_scope=True):
    tok0, L = SEQS[si]
    NCH = L // 128
    sample = (si == 0)
    nh = 8 if kind == "ssd" else 4
    ssd, gdn, ret = kind == "ssd", kind == "gdn", kind == "ret"
    FM, TM, SM, YT = D["FM%d" % l], D["TM%d" % l], D["SM%d" % l], D["YT%d" % l]
    pp = C["pp"][l]
    m = k.mark() if own_scope else None
    PS = PsumPool(k)
    M = load_masks(k, D)
    identf, identb, onesf, onesb = C["identf"], C["identb"], C["onesf"], C["onesb"]
    tsl = slice(tok0, tok0 + L)

    def tm_load(name, col0, ncols):
        t = k.alloc(name, [NCH, ncols], BF16)
        k.dma(t, TM[tsl, col0:col0 + ncols].r("(c p) x -> p c x", p=128))
        return t

    def fm_load(name, row0, ntile):
        t = k.alloc(name, [ntile, L], BF16)
        for i in range(ntile):
            k.dma(t[:, i, :], FM[row0 + 128 * i:row0 + 128 * (i + 1), tsl])
        return t

    sm = k.alloc("sm", [NCH, 32], F32)
    k.dma(sm, SM[tsl, :].r("(c p) x -> p c x", p=128))
    if ssd:
        qT = fm_load("qT", 1280, 2)
        kT = fm_load("kT", 1024, 2)
        k_tm = tm_load("k_tm", 1536, 256)
        x_tm = tm_load("x_tm", 1024, 512)
    elif gdn:
        qT = fm_load("qT", 1536, 4)
        kT = fm_load("kT", 2048, 4)
        k_tm = tm_load("k_tm", 1792, 512)
        v_tm = tm_load("v_tm", 2304, 512)
    else:
        q_tm = tm_load("q_tm", TM_RQ, 512)
        k_tm = tm_load("k_tm", TM_RK, 512)
        v_tm = tm_load("v_tm", TM_RV, 512)
        qkT = [k.alloc("qkT%d" % i, [8, 128], BF16) for i in range(2)]
    nd = 2 * nh
    if ssd or gdn:
        dtb_col = PP_SSD_DTB if ssd else PP_GDN_DTB
        alog_col = PP_SSD_ALOG if ssd else PP_GDN_ALOG
        src0 = 0 if ssd else 24
        nAr = k.alloc("nAr", [nd], F32)
        k.act(nAr, pp[:, alog_col:alog_col + nd], AF.Exp)
        k.ts(nAr, nAr, -1.0, ALU.mult)
        dt = k.alloc("dt", [NCH, nd], F32)
        G = k.alloc("G", [NCH, nd], F32)
        k.tt(dt, sm[:, :, src0:src0 + nd], pp[:, dtb_col:dtb_col + nd].us(1).tb([128, NCH, nd]), ALU.add)
        k.act(dt, dt, AF.Exp)
        k.act(dt, dt, AF.Ln, bias=1.0)
        k.tt(G, dt, nAr.us(1).tb([128, NCH, nd]), ALU.mult)
        if gdn:
            beta = k.alloc("beta", [NCH, 8], F32)
            k.act(beta, sm[:, :, 16:24], AF.Sigmoid)
    else:
        Gc = k.alloc("Gc", [8], F32)
        k.act(Gc, pp[:, PP_RET_DEC:PP_RET_DEC + 8], AF.Exp)
        k.ts(Gc, Gc, -1.0, ALU.mult)
    acc = k.alloc("acc", [4, L], F32)
    visited = [False] * NCH
    Vd = 64 if ssd else 128
    SA = [k.alloc("SA%d" % d, [nh, Vd], F32) for d in range(2)]
    SbA = [k.alloc("SbA%d" % d, [nh, 128], BF16) for d in range(2)]
    S = [SA[i // nh][:, i % nh, :] for i in range(nd)]
    Sb = [SbA[i // nh][:, i % nh, :] for i in range(nd)]
    for i in range(nd):
        h = i % nh
        d = i // nh
        if ssd:
            k.memset(Sb[i], 0.0, eng="pool")
        lv = Sb[i][:, (h % 2) * 64:(h % 2) * 64 + 64] if ssd else Sb[i]
        if sample:
            if ssd:
                tmp = k.alloc("st_tmp%d" % i, [128], F32, parts=64)
                k.dma(tmp, D["st_ssd"][l, d, h])
                r = PS.s()
                k.tr(r[:, 0:64], tmp, identf[0:64, 0:64])
                k.copy(S[i], r[:, 0:64], eng="act")
            else:
                k.dma(S[i], D["st_gdn" if gdn else "st_ret"][l, d, h])
            k.copy(lv, S[i], eng="pool")
        else:
            k.memset(S[i], 0.0)
            if not ssd:
                k.memset(Sb[i], 0.0, eng="pool")
    W = []
    A = {}
    for d in range(2):
        for n in ("diff", "erow", "DTm"):
            A[(n, d)] = k.alloc("%sA%d" % (n, d), [nh, 128], F32)
        for n in ("qdT", "kdec", "attnT"):
            A[(n, d)] = k.alloc("%sA%d" % (n, d), [nh, 128], BF16)
        if gdn:
            A[("tmpA", d)] = k.alloc("tmpAA%d" % d, [nh, 128], F32)
            for n in ("Xp", "Ta", "TTa", "Tb", "TTb", "R", "val"):
                A[(n, d)] = k.alloc("%sA%d" % (n, d), [nh, 128], BF16)
            A[("vb", d)] = k.alloc("vbA%d" % d, [nh, 128], F32)
    for i in range(nd):
        w = {}
        d_, h_ = i // nh, i % nh
        for n in ("diff", "erow", "DTm", "qdT", "kdec", "attnT"):
            w[n] = A[(n, d_)][:, h_, :]
        if gdn:
            w["tmpA"] = A[("tmpA", d_)][:, h_, :]
            w["nbe"] = k.alloc("nbe%d" % i, [128], F32)
            w["NkTall"] = k.alloc("NkTall%d" % i, [7, 128], BF16)
            for n in ("Xp", "Ta", "TTa", "Tb", "TTb", "R", "val"):
                w[n] = A[(n, d_)][:, h_, :]
            for n in ("B0T", "kdTn"):
                w[n] = k.alloc("%s%d" % (n, i), [128], BF16)
        W.append(w)
    ncum = [k.alloc("ncum%d" % d, [nh], F32) for d in range(2)]
    rhsG = [k.alloc("rhsG%d" % i, [4, 128], F32) for i in range(2)]
    if gdn:
        rhsB = [k.alloc("rhsB%d" % i, [4, 128], F32) for i in range(2)]
        nbeta = k.alloc("nbeta", [NCH, 8], F32)
        k.ts(nbeta, beta, -1.0, ALU.mult)
    if ssd:
        valpad = [k.alloc("valpad%d" % d, [8, 128], BF16) for d in range(2)]
        for d in range(2):
            k.memset(valpad[d], 0.0, eng="pool")
    rcnt = 0
    ret_cached = False
    info = {}
    bank_of = {}
    bc_of = {}
    brow_of = {}
    yield
    for s in range(NCH):
        for d in range(2):
            c = s if d == 0 else NCH - 1 - s
            csl = slice(c * 128, (c + 1) * 128)
            mask = M["maskF"] if d == 0 else M["maskB"]
            nmaskS = M["nmaskSF"] if d == 0 else M["nmaskSB"]
            last = 127 if d == 0 else 0
            if ret and ret_cached:
                pass
            else:
                Gd = Gc[:, d * 4:(d + 1) * 4] if ret else G[:, c, d * nh:(d + 1) * nh]
                cr = PS.s()
                k.mm(cr[:, 0:nh], mask, Gd)
                k.act(ncum[d], cr[:, 0:nh], AF.Copy, scale=-1.0)
                for b in range(nh // 4):
                    rg = rhsG[rcnt % 2]
                    rcnt += 1
                    k.tt(rg, mask.us(1).tb([128, 4, 128]), Gd[:, b * 4:b * 4 + 4].us(2).tb([128, 4, 128]), ALU.mult, eng="pool")
                    bank = PS.w()
                    k.mm(bank, onesf, rg.r("p a b -> p (a b)"))
                    bank_of[(d, b)] = bank
                    for hh in range(4):
                        bc_of[(d, b * 4 + hh)] = bank[:, hh * 128:(hh + 1) * 128]
                if gdn:
                    rb_ = rhsB[d]
                    for hh in range(4):
                        k.ts(rb_[:, hh, :], identf, beta[:, c, d * 4 + hh:d * 4 + hh + 1], ALU.mult, eng="pool")
                    bank = PS.w()
                    k.mm(bank, onesf, rb_.r("p a b -> p (a b)"))
                    for hh in range(4):
                        brow_of[(d, hh)] = bank[:, hh * 128:(hh + 1) * 128]
            if ret:
                qk = qkT[(2 * s + d) % 2]
                bank = PS.w().bc(BF16)
                for hh in range(4):
                    k.tr(bank[:, hh * 128:(hh + 1) * 128], q_tm[:, c, hh * 128:(hh + 1) * 128], identb)
                    k.tr(bank[:, (4 + hh) * 128:(5 + hh) * 128], k_tm[:, c, hh * 128:(hh + 1) * 128], identb)
                k.copy(qk, bank.r("p (a b) -> p a b", a=8), eng="act")
            if ssd:
                xv = x_tm[:, c, :].r("p (t e x) -> p t e x", t=4, e=2)
                dv = dt[:, c, d * 8:(d + 1) * 8].r("p (t e) -> p t e", e=2)
                vp = valpad[d].r("p (t e) (f x) -> p t e f x", e=2, f=2)
                for e in range(2):
                    k.tt(vp[:, :, e, e, :], xv[:, :, e, :], dv[:, :, e].us(2).tb([128, 4, 64]), ALU.mult,
                         eng=("dve" if e == 0 else "pool"))
            for h in range(nh):
                i = d * nh + h
                if ssd:
                    qT_h = qT[:, h // 4, csl]
                    kT_h = kT[:, h // 4, csl]
                    ktm_h = k_tm[:, c, (h // 4) * 128:(h // 4 + 1) * 128]
                elif gdn:
                    qT_h = qT[:, h, csl]
                    kT_h = kT[:, h, csl]
                    ktm_h = k_tm[:, c, h * 128:(h + 1) * 128]
                else:
                    qT_h = qk[:, h, :]
                    kT_h = qk[:, 4 + h, :]
                    ktm_h = k_tm[:, c, h * 128:(h + 1) * 128]
                info[i] = dict(d=d, h=h, c=c, qT=qT_h, kT=kT_h, ktm=ktm_h, mask=mask, nmaskS=nmaskS, last=last,
                               qkbuf=(qk if ret else None))
        ALL = range(nd)
        if not (ret and ret_cached):
            for d in range(2):
                mask = M["maskF"] if d == 0 else M["maskB"]
                for b in range(nh // 4):
                    b4 = slice(b * 4, b * 4 + 4)
                    bank3 = bank_of[(d, b)].r("p (a x) -> p a x", a=4)
                    df = A[("diff", d)][:, b4, :]
                    k.tt(df, bank3, ncum[d][:, b4].us(2).tb([128, 4, 128]), ALU.add)
                    k.ts(df, df, 0.0, ALU.min)
                    k.act(df, df, AF.Exp)
                    k.act(A[("erow", d)][:, b4, :], bank3, AF.Exp)
                    k.tt(A[("DTm", d)][:, b4, :], df, mask.us(1).tb([128, 4, 128]), ALU.mult, eng="pool")
        for d in range(2):
            for b in range(nh // 4):
                kqb = k.banks[4 + (2 * d + b) % 4]
                for hh in range(4):
                    i = d * nh + b * 4 + hh
                    k.mm(kqb[:, hh * 128:(hh + 1) * 128], info[i]["kT"], info[i]["qT"])
                b4 = slice(b * 4, b * 4 + 4)
                k.tt(A[("attnT", d)][:, b4, :], kqb.r("p (a x) -> p a x", a=4), A[("DTm", d)][:, b4, :], ALU.mult)
        for d in range(2):
            c = s if d == 0 else NCH - 1 - s
            csl = slice(c * 128, (c + 1) * 128)
            last = 127 if d == 0 else 0
            if ssd:
                for g in range(2):
                    g4 = slice(g * 4, g * 4 + 4)
                    k.tt(A[("qdT", d)][:, g4, :], qT[:, g, csl].us(1).tb([128, 4, 128]), A[("erow", d)][:, g4, :], ALU.mult, eng="pool")
                    k.tt(A[("kdec", d)][:, g4, :], k_tm[:, c, g * 128:(g + 1) * 128].us(1).tb([128, 4, 128]),
                         A[("DTm", d)][:, g4, last:last + 1].tb([128, 4, 128]), ALU.mult, eng="pool")
            else:
                qsrc = qT[:, :, csl] if gdn else info[d * nh]["qkbuf"][:, 0:4, :]
                k.tt(A[("qdT", d)], qsrc, A[("erow", d)], ALU.mult, eng="pool")
                k.tt(A[("kdec", d)], k_tm[:, c, :].r("p (a x) -> p a x", a=4),
                     A[("DTm", d)][:, :, last:last + 1].tb([128, 4, 128]), ALU.mult, eng="pool")
        if gdn:
            kks = {}
            for i in ALL:
                kks[i] = PS.s()
                k.mm(kks[i], info[i]["kT"], info[i]["kT"])
            for i in ALL:
                k.tt(W[i]["tmpA"], kks[i], W[i]["diff"], ALU.mult)
            for d in range(2):
                nm = M["nmaskSF"] if d == 0 else M["nmaskSB"]
                k.tt(A[("tmpA", d)], A[("tmpA", d)], nm.us(1).tb([128, 4, 128]), ALU.mult, eng="pool")
            for i in ALL:
                f = info[i]
                brow = brow_of[(f["d"], f["h"])]
                k.tt(W[i]["B0T"], brow, W[i]["tmpA"], ALU.mult)
                k.tt(W[i]["nbe"], brow, W[i]["erow"], ALU.mult)
                k.stt(W[i]["kdTn"], W[i]["nbe"], -1.0, f["kT"], ALU.mult, ALU.mult)
            for i in ALL:
                m0 = 7 if info[i]["d"] == 0 else 0
                k.tt(W[i]["NkTall"], W[i]["B0T"].us(1).tb([128, 7, 128]), M["blk"][:, m0:m0 + 7, :], ALU.mult, eng="pool")
            Tc = {i: (identb, identb) for i in ALL}
            TcA = {0: None, 1: None}
            id3 = identb.us(1).tb([128, 4, 128])
            bcnt3 = 0
            for lev in range(7):
                for d in range(2):
                    p1b = k.banks[4 + bcnt3 % 4]
                    p2b = k.banks[4 + (bcnt3 + 1) % 4]
                    p3b = k.banks[4 + (bcnt3 + 2) % 4]
                    bcnt3 += 3
                    for hh in range(4):
                        i = d * 4 + hh
                        k.mm(p1b[:, hh * 128:(hh + 1) * 128], W[i]["NkTall"][:, lev, :], Tc[i][0])
                    k.copy(A[("Xp", d)].r("p a x -> p (a x)"), p1b, eng="act")
                    for hh in range(4):
                        i = d * 4 + hh
                        if lev < 6:
                            k.mm(p2b[:, hh * 128:(hh + 1) * 128], Tc[i][1], W[i]["Xp"])
                        k.mm(p3b[:, hh * 128:(hh + 1) * 128], W[i]["Xp"], Tc[i][1])
                    nT, nTT = ("Ta", "TTa") if lev % 2 == 0 else ("Tb", "TTb")
                    curT = id3 if TcA[d] is None else A[(TcA[d][0], d)]
                    curTT = id3 if TcA[d] is None else A[(TcA[d][1], d)]
                    if lev < 6:
                        k.tt(A[(nT, d)], p2b.r("p (a x) -> p a x", a=4), curT, ALU.add)
                    k.tt(A[(nTT, d)], p3b.r("p (a x) -> p a x", a=4), curTT, ALU.add)
                    TcA[d] = (nT, nTT)
                    for hh in range(4):
                        i = d * 4 + hh
                        Tc[i] = (W[i][nT], W[i][nTT])
            for i in ALL:
                W[i]["Pb"] = Tc[i][1]
        if ret:
            ret_cached = True
        vals = {}
        rbc = [0]

        def nbank():
            b = k.banks[4 + rbc[0] % 4]
            rbc[0] += 1
            return b

        if gdn:
            rbs = [nbank(), nbank()]
            vbs = [nbank(), nbank()]
            for d in range(2):
                c = s if d == 0 else NCH - 1 - s
                for hh in range(4):
                    i = d * 4 + hh
                    k.mm(rbs[d][:, hh * 128:(hh + 1) * 128], W[i]["kdTn"], Sb[i])
                k.tt(A[("vb", d)], v_tm[:, c, :].r("p (a x) -> p a x", a=4),
                     beta[:, c, d * 4:(d + 1) * 4].us(2).tb([128, 4, 128]), ALU.mult, eng="pool")
            for d in range(2):
                k.tt(A[("R", d)], rbs[d].r("p (a x) -> p a x", a=4), A[("vb", d)], ALU.add)
            for d in range(2):
                for hh in range(4):
                    i = d * 4 + hh
                    k.mm(vbs[d][:, hh * 128:(hh + 1) * 128], W[i]["Pb"], W[i]["R"])
            for d in range(2):
                k.copy(A[("val", d)].r("p a x -> p (a x)"), vbs[d], eng="act")
            for i in ALL:
                vals[i] = (W[i]["val"], W[i]["val"])
        elif ret:
            for i in ALL:
                f = info[i]
                vals[i] = (v_tm[:, f["c"], f["h"] * 128:(f["h"] + 1) * 128],) * 2
        else:
            for i in ALL:
                f = info[i]
                h = f["h"]
                vals[i] = (valpad[f["d"]][:, h, :], valpad[f["d"]][:, h, (h % 2) * 64:(h % 2) * 64 + 64])
        obs = [nbank(), nbank()]
        snbs = [nbank(), nbank()]
        for d in range(2):
            for t in range(4):
                hs = (2 * t, 2 * t + 1) if ssd else (t,)
                reg = obs[d][:, t * 128:(t + 1) * 128]
                for e, h in enumerate(hs):
                    i = d * nh + h
                    k.mm(reg, Sb[i], W[i]["qdT"], start=(e == 0), stop=False)
                    k.mm(reg, vals[i][0], W[i]["attnT"], start=False, stop=(e == len(hs) - 1))
        for d in range(2):
            for h in range(nh):
                i = d * nh + h
                k.mm(snbs[d][:, h * Vd:(h + 1) * Vd], W[i]["kdec"], vals[i][1])
        for d in range(2):
            c = s if d == 0 else NCH - 1 - s
            dst = acc[:, :, slice(c * 128, (c + 1) * 128)]
            ob3 = obs[d].r("p (a x) -> p a x", a=4)
            if not visited[c]:
                k.copy(dst, ob3, eng="act")
            else:
                k.tt(dst, ob3, dst, ALU.add)
            visited[c] = True
        for d in range(2):
            last = 127 if d == 0 else 0
            k.tt(SA[d], SA[d], A[("erow", d)][:, :, last:last + 1].tb([128, nh, Vd]), ALU.mult, eng="pool")
        for d in range(2):
            k.tt(SA[d], snbs[d].r("p (a x) -> p a x", a=nh), SA[d], ALU.add)
        for d in range(2):
            if ssd:
                sbv = SbA[d].r("p (t e) (f x) -> p t e f x", e=2, f=2)
                sav = SA[d].r("p (t e) x -> p t e x", e=2)
                for e in range(2):
                    k.copy(sbv[:, :, e, e, :], sav[:, :, e, :], eng="pool")
            else:
                k.copy(SbA[d], SA[d], eng="pool")
        yield
    if not sample:
        pi = si - 1
        for i in range(nd):
            h = i % nh
            d = i // nh
            if ssd:
                r = PS.s()
                k.tr(r[0:64, :], S[i], identf)
                tmp = k.alloc("so_tmp%d" % i, [128], F32, parts=64)
                k.copy(tmp, r[0:64, :], eng="act")
                k.dma(D["ns_ssd"][pi, l, d, h], tmp)
            else:
                k.dma(D["ns_gdn" if gdn else "ns_ret"][pi, l, d, h], S[i])
    yrow0 = {"ssd": 512, "gdn": 1024, "ret": 1536}[kind]
    gate_row = {"ssd": 2560, "gdn": 3072, "ret": 3584}[kind]
    NB = (L + 511) // 512
    BW = min(512, L)
    gt = [k.alloc("gate%d" % i, [BW], BF16) for i in range(2)]
    yw = [k.alloc("yw%d" % i, [BW], F32) for i in range(4)]
    ysq = [k.alloc("ysq%d" % i, [BW], BF16) for i in range(2)]
    rn = k.alloc("rn", [BW], F32)
    yo = [k.alloc("yo%d" % i, [BW], BF16) for i in range(2)]
    if ssd:
        xg = [k.alloc("xg%d" % i, [BW], BF16) for i in range(2)]
        dsum = k.alloc("dsum", [4], F32)
        k.tt(dsum, pp[:, PP_SSD_DCOL:PP_SSD_DCOL + 4], pp[:, PP_SSD_DCOL + 4:PP_SSD_DCOL + 8], ALU.add)
    cnt = 0
    for nb in range(NB):
        bsl = slice(nb * BW, (nb + 1) * BW)
        gsl = slice(tok0 + nb * BW, tok0 + (nb + 1) * BW)
        if ssd:
            ssb = PS.w()
            for t in range(4):
                g_ = gt[cnt % 2]
                x_ = xg[cnt % 2]
                cnt += 1
                k.dma(g_, FM[gate_row + 128 * t:gate_row + 128 * (t + 1), gsl])
                k.dma(x_, FM[512 + 128 * t:512 + 128 * (t + 1), gsl])
                y = yw[t]
                k.stt(y, x_, dsum[:, t:t + 1], acc[:, t, bsl], ALU.mult, ALU.add)
                k.tt(y, y, g_, ALU.mult, eng="pool")
                sq_ = ysq[t % 2]
                k.tt(sq_, y, y, ALU.mult, eng="pool")
                k.mm(ssb[:, 0:BW], onesb, sq_, start=(t == 0), stop=(t == 3))
            k.act(rn, ssb[:, 0:BW], AF.Sqrt, bias=1e-6, scale=1.0 / 512)
            k.recip(rn, rn)
            for t in range(4):
                o_ = yo[t % 2]
                k.stt(o_, yw[t], pp[:, PP_SSDNW + t:PP_SSDNW + t + 1], rn, ALU.mult, ALU.mult)
                k.dma(YT[yrow0 + 128 * t:yrow0 + 128 * (t + 1), gsl], o_)
        else:
            for t in range(4):
                g_ = gt[cnt % 2]
                cnt += 1
                k.dma(g_, FM[gate_row + 128 * t:gate_row + 128 * (t + 1), gsl])
                a_ = acc[:, t, bsl]
                y = yw[t % 2]
                if ret:
                    mb = PS.w()
                    k.mm(mb[:, 0:BW], onesf, a_)
                    k.stt(y, mb[:, 0:BW], -1.0 / 128, a_, ALU.mult, ALU.add)
                    src = y
                else:
                    src = a_
                sq_ = ysq[t % 2]
                k.tt(sq_, src, src, ALU.mult, eng="pool")
                ssb = PS.w()
                k.mm(ssb[:, 0:BW], onesb, sq_)
                k.act(rn, ssb[:, 0:BW], AF.Sqrt, bias=(1e-5 if ret else 1e-6), scale=1.0 / 128)
                k.recip(rn, rn)
                o_ = yo[t % 2]
                if ret:
                    k.tt(y, src, rn, ALU.mult)
                    k.tt(o_, y, g_, ALU.mult, eng="pool")
                else:
                    k.stt(y, src, pp[:, PP_GDNNW:PP_GDNNW + 1], rn, ALU.mult, ALU.mult)
                    k.tt(o_, y, g_, ALU.mult, eng="pool")
                k.dma(YT[yrow0 + 128 * t:yrow0 + 128 * (t + 1), gsl], o_)
    if own_scope:
        k.release(m)
    yield


def la_mixer(k, D, C, l, kind, si):
    for _ in la_mixer_gen(k, D, C, l, kind, si):
        pass


def la_mixer_pair(k, D, C, l, specs):
    m = k.mark()
    gens = [la_mixer_gen(k, D, C, l, kind, si, own_scope=False) for (kind, si) in specs]
    live = list(gens)
    while live:
        for g in list(live):
            try:
                next(g)
            except StopIteration:
                live.remove(g)
    k.release(m)
I32 = mybir.dt.int32
DSIZE[I32] = 4
TWO_PI = 2.0 * math.pi


def hyena_consts(L):
    bf = ml_dtypes.bfloat16
    N = 2 * L
    LT = L // 128
    FT = N // 128
    n = np.arange(L, dtype=np.float64)
    f = np.arange(L, dtype=np.float64)
    ang = 2.0 * np.pi * np.outer(n, f) / N
    Fk = np.zeros((N, N), np.float64)
    Fk[:L, :L] = np.cos(ang)
    Fk[:L, L] = np.cos(np.pi * n)
    Fk[:L, L + 1:] = -np.sin(ang[:, 1:])
    Fk[L:, :L] = np.cos(ang)
    Fk[L:, L] = np.cos(np.pi * n)
    Fk[L:, L + 1:] = np.sin(ang[:, 1:])
    Fk[L, :] = 0.0
    FkTt = Fk.reshape(FT, 128, FT, 128).transpose(2, 1, 0, 3)
    t = np.arange(L, dtype=np.float64)
    G = np.zeros((N, L), np.float64)
    angt = 2.0 * np.pi * np.outer(f, t) / N
    G[:L, :] = (2.0 / N) * np.cos(angt)
    G[0, :] = 1.0 / N
    G[L, :] = (1.0 / N) * np.cos(np.pi * t)
    G[L + 1:, :] = -(2.0 / N) * np.sin(angt[1:, :])
    GTa = G.reshape(FT, 128, LT, 128).transpose(2, 1, 0, 3)
    BW = min(512, L)
    GTb = G.reshape(FT, 128, L // BW, BW).transpose(2, 1, 0, 3)
    pos = np.arange(L, dtype=np.float32)
    tn = pos / np.float32(L - 1)
    bands = np.linspace(1e-4, 7, 8, dtype=np.float32)
    a2 = (np.float32(2.0 * math.pi / L) * pos[:, None]) * bands
    feats = np.concatenate([tn[:, None], np.cos(a2), -np.sin(a2)], -1).astype(np.float32)
    deltas = np.abs(np.linspace(math.log(1e-2) / 1.5, math.log(1e-2) / 0.3, 512, dtype=np.float32))
    dec = np.exp(-tn[:, None] * deltas).astype(np.float32)
    return {"FkTt%d" % L: np.ascontiguousarray(FkTt).astype(bf), "GTa%d" % L: np.ascontiguousarray(GTa).astype(bf),
            "GTb%d" % L: np.ascontiguousarray(GTb).astype(bf), "featsT%d" % L: np.ascontiguousarray(feats.T),
            "dec%d" % L: dec}


def declare_hyena(k, D, dbg):
    for L in (2048, 256):
        LT, FT, BW = L // 128, 2 * L // 128, min(512, L)
        D["FkTt%d" % L] = k.dram("FkTt%d" % L, [FT, 128, FT, 128], BF16, "ExternalInput")
        D["GTa%d" % L] = k.dram("GTa%d" % L, [LT, 128, FT, 128], BF16, "ExternalInput")
        D["GTb%d" % L] = k.dram("GTb%d" % L, [L // BW, 128, FT, BW], BF16, "ExternalInput")
        D["featsT%d" % L] = k.dram("featsT%d" % L, [17, L], F32, "ExternalInput")
        D["dec%d" % L] = k.dram("dec%d" % L, [L, 512], F32, "ExternalInput")
        for l in range(2):
            n = "KH%d_%d" % (l, L)
            D[n] = k.dram(n, [2 * L, 1024], F32, "ExternalOutput" if n in dbg else "Internal")


def hyena_filter(k, D, C, l, L):
    LT, FT = L // 128, 2 * L // 128
    NB = max(1, L // 512)
    BW = min(512, L)
    pp = C["pp"][l]
    m = k.mark()
    PS = PsumPool(k)
    w1 = k.alloc("hw1", [64], F32, parts=17)
    w2 = k.alloc("hw2", [64], F32, parts=64)
    w3 = k.alloc("hw3", [2048], F32, parts=64)
    fT = k.alloc("fT", [L], F32, parts=17)
    k.dma(w1, D["hy_w1"][l]); k.dma(w2, D["hy_w2"][l]); k.dma(w3, D["hy_w3"][l]); k.dma(fT, D["featsT%d" % L])
    hT = [k.alloc("hT%d" % i, [L], F32, parts=64) for i in range(2)]
    fb = k.alloc("fb", [2], F32, parts=64)
    k.tt(fb[:, 0:1], pp[0:64, PP_HYF0:PP_HYF0 + 1], pp[0:64, PP_HYB1:PP_HYB1 + 1], ALU.mult)
    k.tt(fb[:, 1:2], pp[0:64, PP_HYF1:PP_HYF1 + 1], pp[0:64, PP_HYB2:PP_HYB2 + 1], ALU.mult)
    arg = k.alloc("harg", [BW], F32, parts=64)
    ni = k.alloc("hni", [BW], I32, parts=64)
    nf = k.alloc("hnf", [BW], F32, parts=64)
    for layer in range(2):
        fcol = PP_HYF0 if layer == 0 else PP_HYF1
        for nb in range(NB):
            bsl = slice(nb * BW, (nb + 1) * BW)
            bank = PS.w()
            if layer == 0:
                k.mm(bank[0:64, 0:BW], w1, fT[:, bsl])
            else:
                k.mm(bank[0:64, 0:BW], w2, hT[0][:, bsl])
            k.ts(arg, bank[0:64, 0:BW], pp[0:64, fcol:fcol + 1], ALU.mult, fb[:, layer:layer + 1], ALU.add)
            k.ts(nf, arg, 1.0 / TWO_PI, ALU.mult)
            k.copy(ni, nf)
            k.copy(nf, ni)
            k.stt(arg, nf, -TWO_PI, arg, ALU.mult, ALU.add)
            k.act(hT[layer][:, bsl], arg, AF.Sin)
    h2T = hT[1]
    kernb = k.alloc("kernb", [2 * LT, 1024], BF16)
    dect = [k.alloc("dect%d" % i, [512], F32) for i in range(2)]
    absb = [k.alloc("absb%d" % i, [512], BF16) for i in range(2)]
    Sb0, Sb1 = PS.w(), PS.w()
    cnt = 0
    for tt in range(LT):
        dc = dect[tt % 2]
        k.dma(dc, D["dec%d" % L][tt * 128:(tt + 1) * 128, :])
        for cbk in range(4):
            o, d = cbk // 2, cbk % 2
            p = PS.s() if False else k.banks[4 + cnt % 4]
            k.mm(p, h2T[:, tt * 128:(tt + 1) * 128], w3[:, cbk * 512:(cbk + 1) * 512])
            dst = kernb[:, d * LT + tt, o * 512:(o + 1) * 512]
            k.tt(dst, p, dc, ALU.mult)
            if d == 1 and tt == 0:
                k.memset(kernb[0:1, LT, o * 512:(o + 1) * 512], 0.0)
            ab = absb[cnt % 2]
            k.act(ab, dst, AF.Abs)
            first = (tt == 0 and d == 0)
            lastt = (tt == LT - 1 and d == 1)
            k.mm(Sb0 if o == 0 else Sb1, C["onesb"], ab, start=first, stop=lastt)
            cnt += 1
    sinv = k.alloc("sinv", [1024], F32)
    k.recip(sinv[:, 0:512], Sb0)
    k.recip(sinv[:, 512:1024], Sb1)
    hyb = k.alloc("hyb", [1024], F32)
    k.dma(hyb, D["hy_bias"][l, :].pb(128))
    fk = [k.alloc("fk%d" % i, [2 * LT, 128], BF16) for i in range(2)]
    kh = [k.alloc("kh%d" % i, [1024], F32) for i in range(2)]
    KH = D["KH%d_%d" % (l, L)]
    for pt in range(FT):
        f_ = fk[pt % 2]
        k.dma(f_, D["FkTt%d" % L][pt])
        b0, b1 = k.banks[(2 * pt) % 4], k.banks[(2 * pt + 1) % 4]
        for lt in range(2 * LT):
            k.mm(b0, f_[:, lt, :], kernb[:, lt, 0:512], start=(lt == 0), stop=(lt == 2 * LT - 1))
        for lt in range(2 * LT):
            k.mm(b1, f_[:, lt, :], kernb[:, lt, 512:1024], start=(lt == 0), stop=(lt == 2 * LT - 1))
        kk_ = kh[pt % 2]
        k.tt(kk_[:, 0:512], b0, sinv[:, 0:512], ALU.mult)
        k.tt(kk_[:, 512:1024], b1, sinv[:, 512:1024], ALU.mult)
        if pt < FT // 2:
            k.tt(kk_, kk_, hyb, ALU.add, eng="pool")
        elif pt == FT // 2:
            k.tt(kk_[0:1, :], kk_[0:1, :], hyb[0:1, :], ALU.add, eng="pool")
        k.dma(KH[pt * 128:(pt + 1) * 128, :], kk_)
    k.release(m)


def hyena_conv(k, D, C, l, si):
    tok0, L = SEQS[si]
    LT, FT = L // 128, 2 * L // 128
    BW = min(512, L)
    NB = L // BW
    FM, TM, YT = D["FM%d" % l], D["TM%d" % l], D["YT%d" % l]
    KH = D["KH%d_%d" % (l, L)]
    m = k.mark()
    tsl = slice(tok0, tok0 + L)
    v_tm = k.alloc("hv", [LT, 512], BF16)
    x1_tm = k.alloc("hx1", [LT, 512], BF16)
    z1 = k.alloc("hz1", [LT, 512], BF16)
    k.dma(v_tm, TM[tsl, 0:512].r("(c p) x -> p c x", p=128))
    k.dma(x1_tm, TM[tsl, 512:1024].r("(c p) x -> p c x", p=128))
    Y = k.alloc("hY", [FT, 512], BF16)
    fkr = [k.alloc("fkr%d" % i, [LT, 128], BF16) for i in range(2)]
    fki = [k.alloc("fki%d" % i, [LT, 128], BF16) for i in range(2)]
    Kr = [k.alloc("Kr%d" % i, [512], F32) for i in range(2)]
    Ki = [k.alloc("Ki%d" % i, [512], F32) for i in range(2)]
    zr = k.alloc("zr", [512], F32)
    zi = k.alloc("zi", [512], F32)
    t1 = k.alloc("ht1", [512], F32)
    t2 = k.alloc("ht2", [512], F32)
    t3 = k.alloc("ht3", [512], F32)
    t4 = k.alloc("ht4", [512], F32)
    H = FT // 2
    cnt = 0
    for order in range(2):
        z = v_tm if order == 0 else z1
        osl = slice(order * 512, (order + 1) * 512)
        for pq in range(H):
            fr, fi = fkr[cnt % 2], fki[cnt % 2]
            kr, ki = Kr[cnt % 2], Ki[cnt % 2]
            cnt += 1
            k.dma(fr, D["FkTt%d" % L][pq][:, 0:LT, :])
            k.dma(fi, D["FkTt%d" % L][pq + H][:, 0:LT, :])
            k.dma(kr, KH[pq * 128:(pq + 1) * 128, osl])
            k.dma(ki, KH[(pq + H) * 128:(pq + H + 1) * 128, osl])
            br, bi = k.banks[(2 * pq) % 4], k.banks[(2 * pq + 1) % 4]
            for lt in range(LT):
                k.mm(br, fr[:, lt, :], z[:, lt, :], start=(lt == 0), stop=(lt == LT - 1))
            for lt in range(LT):
                k.mm(bi, fi[:, lt, :], z[:, lt, :], start=(lt == 0), stop=(lt == LT - 1))
            k.copy(zr, br, eng="act")
            k.copy(zi, bi, eng="act")
            k.tt(t1, zr, kr, ALU.mult)
            k.tt(t2, zi, ki, ALU.mult, eng="pool")
            k.tt(Y[:, pq, :], t1, t2, ALU.subtract)
            k.tt(t3, zr, ki, ALU.mult)
            k.tt(t4, zi, kr, ALU.mult, eng="pool")
            k.tt(Y[:, pq + H, :], t3, t4, ALU.add, eng="pool")
            if pq == 0:
                k.tt(Y[0:1, 0, :], zr[0:1, :], kr[0:1, :], ALU.mult)
                k.tt(Y[0:1, H, :], zi[0:1, :], ki[0:1, :], ALU.mult)
        if order == 0:
            ga = [k.alloc("ga%d" % i, [FT, 128], BF16) for i in range(2)] if si >= 0 else None
            for tt in range(LT):
                g_ = ga[tt % 2]
                k.dma(g_, D["GTa%d" % L][tt])
                bank = k.banks[4 + tt % 4]
                for pt in range(FT):
                    k.mm(bank, g_[:, pt, :], Y[:, pt, :], start=(pt == 0), stop=(pt == FT - 1))
                k.tt(z1[:, tt, :], bank, x1_tm[:, tt, :], ALU.mult)
        else:
            gb = k.alloc("gb", [FT, BW], BF16)
            x2 = [k.alloc("hx2%d" % i, [BW], BF16) for i in range(2)]
            yo = [k.alloc("hyo%d" % i, [BW], BF16) for i in range(2)]
            c2 = 0
            for nb in range(NB):
                k.dma(gb, D["GTb%d" % L][nb])
                gsl = slice(tok0 + nb * BW, tok0 + (nb + 1) * BW)
                for ct in range(4):
                    bank = k.banks[4 + c2 % 4]
                    x_, o_ = x2[c2 % 2], yo[c2 % 2]
                    c2 += 1
                    k.dma(x_, FM[ct * 128:(ct + 1) * 128, gsl])
                    for pt in range(FT):
                        k.mm(bank[:, 0:BW], Y[:, pt, ct * 128:(ct + 1) * 128], gb[:, pt, :], start=(pt == 0), stop=(pt == FT - 1))
                    k.tt(o_, bank[:, 0:BW], x_, ALU.mult)
                    k.dma(YT[ct * 128:(ct + 1) * 128, gsl], o_)
    k.release(m)
def ln_tile(k, z, lng, lnb, out, st, junk):
    k.act(junk, z, AF.Identity, accum=st[:, 0:1])
    k.ts(st[:, 1:2], st[:, 0:1], -1.0 / DM, ALU.mult)
    k.ts(z, z, st[:, 1:2], ALU.add)
    k.act(junk, z, AF.Square, accum=st[:, 2:3])
    k.act(st[:, 3:4], st[:, 2:3], AF.Sqrt, bias=1e-5, scale=1.0 / DM)
    k.recip(st[:, 3:4], st[:, 3:4])
    k.stt(z, z, st[:, 3:4], lng, ALU.mult, ALU.mult)
    k.tt(out, z, lnb, ALU.add, eng="pool")


def phase_out(k, D, C, l, Xsrc):
    m = k.mark()
    YT, X1 = D["YT%d" % l], D["X1_%d" % l]
    wout = k.alloc("wout", [16, DM], BF16)
    wv = D["w_out"][l].r("(k p) n -> p k n", p=128)
    woutc = [wout.part((slice(None), slice(None), slice(cc * 512, (cc + 1) * 512)), "woutc%d_%d" % (cc, l)) for cc in range(4)]
    for cc in range(4):
        k.dma(woutc[cc], wv[:, :, cc * 512:(cc + 1) * 512], eng="pool")
    ytb = [k.alloc("ytb%d" % i, [16, 512], BF16) for i in range(2)]
    xb = [k.alloc("oxb%d" % i, [DM], F32) for i in range(2)]
    zb = [k.alloc("ozb%d" % i, [DM], F32) for i in range(2)]
    junk = k.alloc("ojunk", [DM], BF16)
    g1 = k.alloc("g1", [DM], F32)
    lng = k.alloc("lng", [DM], F32)
    lnb = k.alloc("lnb", [DM], F32)
    st = [k.alloc("ost%d" % i, [4], F32) for i in range(2)]
    k.dma(lng, D["ln1_g"][l, :].pb(128))
    k.dma(lnb, D["ln1_b"][l, :].pb(128))
    ytv = YT.r("(k p) t -> p k t", p=128)
    for tb in range(5):
        y = ytb[tb % 2]
        k.dma(y, ytv[:, :, tb * 512:(tb + 1) * 512])
        if tb == 0 or tb == 4:
            k.dma(g1, D["modv"][l, 0 if tb == 0 else 1, MOD_G1:MOD_G1 + DM].pb(128))
        for j in range(4):
            tt = tb * 4 + j
            x = xb[tt % 2]
            z = zb[tt % 2]
            k.dma(x, Xsrc[tt * 128:(tt + 1) * 128, :])
            for cc in range(4):
                bank = k.banks[(tt % 2) * 4 + cc]
                for kt in range(16):
                    k.mm(bank, y[:, kt, j * 128:(j + 1) * 128], woutc[cc][:, kt, :],
                         start=(kt == 0), stop=(kt == 15))
                k.tt(z[:, cc * 512:(cc + 1) * 512], bank, g1[:, cc * 512:(cc + 1) * 512], ALU.mult)
            k.stt(z, x, ALPHA, z, ALU.mult, ALU.add)
            ln_tile(k, z, lng, lnb, x, st[tt % 2], junk)
            k.dma(X1[tt * 128:(tt + 1) * 128, :], x)
    k.release(m)


def phase_up(k, D, C, l, uT, uTb):
    m = k.mark()
    pp = C["pp"][l]
    AT = D["AT%d" % l]
    wv = D["w_up"][l].r("(k p) n -> p k n", p=128)
    wg = [k.alloc("wg%d" % i, [16, 256], BF16) for i in range(2)]
    wvl = [k.alloc("wvl%d" % i, [16, 256], BF16) for i in range(2)]
    cb = [k.alloc("ucb%d" % i, [2564], F32) for i in range(2)]
    for c in cb:
        for g in (0, 2049, 2306, 2563):
            k.memset(c[:, g:g + 1], 0.0)
    t1g = k.alloc("t1g", [2562], F32)
    t1v = k.alloc("t1v", [2562], F32)
    sg = k.alloc("sg", [NTOK], F32)
    ob = [k.alloc("uob%d" % i, [NTOK], BF16) for i in range(2)]
    NCH = 22
    bcnt = 0

    def load(ci):
        k.dma(wg[ci % 2], wv[:, :, ci * 256:(ci + 1) * 256], eng="pool")
        k.dma(wvl[ci % 2], wv[:, :, DFF + ci * 256:DFF + (ci + 1) * 256], eng="pool")

    load(0)
    for ci in range(NCH):
        if ci + 1 < NCH:
            load(ci + 1)
        for mi in range(2):
            i = ci * 2 + mi
            res = []
            for which, w, t1 in ((0, wg[ci % 2], t1g), (1, wvl[ci % 2], t1v)):
                banks5 = []
                for tb in range(5):
                    bank = k.banks[bcnt % 8]
                    bcnt += 1
                    for kt in range(16):
                        k.mm(bank, w[:, kt, mi * 128:(mi + 1) * 128], V(uT.ap[:, kt, tb * 512:(tb + 1) * 512], uTb[tb].res),
                             start=(kt == 0), stop=(kt == 15))
                    banks5.append(bank)
                j = i if which == 0 else 44 + i
                pc = pp[:, PP_FFN + 4 * j:PP_FFN + 4 * j + 4]
                conv_epilogue(k, banks5, cb[which], t1, pc)
                res.append(pc)
            o = ob[i % 2]
            for (c0, n, t0) in T1_SEGS:
                k.act(sg[:, t0:t0 + n], t1g[:, c0:c0 + n], AF.Silu, bias=res[0][:, 3:4])
                k.stt(o[:, t0:t0 + n], t1v[:, c0:c0 + n], res[1][:, 3:4], sg[:, t0:t0 + n], ALU.add, ALU.mult)
            k.dma(AT[i * 128:(i + 1) * 128, :], o)
    k.release(m)


def phase_down(k, D, C, l):
    AT, F2 = D["AT%d" % l], D["F2_%d" % l]
    wv = D["w_down"][l].r("(k p) n -> p k n", p=128)
    KT = 44
    for half in range(2):
        m = k.mark()
        h0 = half * 1280
        aT = k.alloc("aT", [KT, 1280], BF16)
        aTk = [aT.part((slice(None), kt, slice(None)), "aTk%d_%d_%d" % (kt, half, l)) for kt in range(KT)]
        for kt in range(KT):
            k.dma(aTk[kt], AT[kt * 128:(kt + 1) * 128, h0:h0 + 1280])
        wd = [k.alloc("wd%d" % i, [KT, 256], BF16) for i in range(2)]
        stg = [k.alloc("dstg%d" % i, [256], F32) for i in range(2)]
        k.dma(wd[0], wv[:, :, 0:256], eng="pool")
        cnt = 0
        for cc in range(8):
            if cc + 1 < 8:
                k.dma(wd[(cc + 1) % 2], wv[:, :, (cc + 1) * 256:(cc + 2) * 256], eng="pool")
            w = wd[cc % 2]
            for tt in range(10):
                bank = k.banks[cnt % 8]
                for kt in range(KT):
                    k.mm(bank[:, 0:256], aTk[kt][:, tt * 128:(tt + 1) * 128], w[:, kt, :], start=(kt == 0), stop=(kt == KT - 1))
                s_ = stg[cnt % 2]
                cnt += 1
                k.copy(s_, bank[:, 0:256], eng=("act" if cnt % 2 else "dve"))
                k.dma(F2[h0 + tt * 128:h0 + (tt + 1) * 128, cc * 256:(cc + 1) * 256], s_)
        k.release(m)


def phase_ln2(k, D, C, l, Xdst):
    m = k.mark()
    F2, X1 = D["F2_%d" % l], D["X1_%d" % l]
    fb = [k.alloc("lfb%d" % i, [DM], F32) for i in range(2)]
    xb = [k.alloc("lxb%d" % i, [DM], F32) for i in range(2)]
    junk = k.alloc("ljunk", [DM], BF16)
    g2 = k.alloc("g2", [DM], F32)
    lng = k.alloc("lng2", [DM], F32)
    lnb = k.alloc("lnb2", [DM], F32)
    st = [k.alloc("lst%d" % i, [4], F32) for i in range(2)]
    k.dma(lng, D["ln2_g"][l, :].pb(128))
    k.dma(lnb, D["ln2_b"][l, :].pb(128))
    for tt in range(NTT):
        if tt == 0 or tt == 16:
            k.dma(g2, D["modv"][l, 0 if tt == 0 else 1, MOD_G2:MOD_G2 + DM].pb(128))
        f = fb[tt % 2]
        x = xb[tt % 2]
        k.dma(f, F2[tt * 128:(tt + 1) * 128, :])
        k.dma(x, X1[tt * 128:(tt + 1) * 128, :])
        k.tt(f, f, g2, ALU.mult, eng="pool")
        k.stt(f, x, ALPHA, f, ALU.mult, ALU.add)
        ln_tile(k, f, lng, lnb, x, st[tt % 2], junk)
        k.dma(Xdst[tt * 128:(tt + 1) * 128, :], x)
    k.release(m)
def build(upto="all", dbg=()):
    k = KB()
    D = declare(k, dbg)
    declare_hyena(k, D, dbg)
    C = load_consts(k, D)
    phase_mod(k, D, C)
    if upto == "mod":
        return k, k.finalize()
    for l in range(2):
        for L in (2048, 256):
            hyena_filter(k, D, C, l, L)
    if upto == "hyf":
        return k, k.finalize()
    for l in range(2):
        Xsrc = D["X0"] if l == 0 else D["X_1"]
        m0 = k.mark()
        uT = k.alloc("uT", [16, NTOK], BF16)
        uTb = [uT.part((slice(None), slice(None), slice(tb * 512, (tb + 1) * 512)), "uTb%d_%d" % (tb, l)) for tb in range(5)]
        front(k, D, C, Xsrc, l, MOD_SC1, MOD_SH1, uT, uTb)
        phase_in(k, D, C, l, uT, uTb)
        k.release(m0)
        if upto == "in":
            return k, k.finalize()
        only = upto if upto in ("hy", "ssd", "gdn", "ret") else None
        if only in (None, "hy"):
            for si in range(3):
                hyena_conv(k, D, C, l, si)
        if only is None:
            for kind in ("ssd", "gdn", "ret"):
                la_mixer(k, D, C, l, kind, 0)
            la_mixer_pair(k, D, C, l, (("ssd", 1), ("gdn", 1)))
            la_mixer_pair(k, D, C, l, (("ssd", 2), ("gdn", 2)))
            la_mixer_pair(k, D, C, l, (("ret", 1), ("ret", 2)))
        else:
            for kind in ("ssd", "gdn", "ret"):
                if only == kind:
                    for si in range(3):
                        la_mixer(k, D, C, l, kind, si)
        if only is not None or upto == "mix":
            return k, k.finalize()
        phase_out(k, D, C, l, Xsrc)
        if upto == "out":
            return k, k.finalize()
        m0 = k.mark()
        uT = k.alloc("uT", [16, NTOK], BF16)
        uTb = [uT.part((slice(None), slice(None), slice(tb * 512, (tb + 1) * 512)), "uTc%d_%d" % (tb, l)) for tb in range(5)]
        front(k, D, C, D["X1_%d" % l], l, MOD_SC2, MOD_SH2, uT, uTb)
        phase_up(k, D, C, l, uT, uTb)
        k.release(m0)
        if upto == "up":
            return k, k.finalize()
        phase_down(k, D, C, l)
        phase_ln2(k, D, C, l, D["X_1"] if l == 0 else D["X_2"])
        if upto == "l0":
            return k, k.finalize()
    return k, k.finalize()


def shared_inputs(inp):
    m = host_shared(inp)
    for L in (2048, 256):
        m.update(hyena_consts(L))
    return m


_CACHE = {}


def kernel(**inputs):
    inp = {k_: np.asarray(v) for k_, v in inputs.items()}
    if "nc" not in _CACHE:
        _CACHE["nc"] = build("all", ())[1]
    nc = _CACHE["nc"]
    shared = shared_inputs(inp)
    in_maps = []
    for core in range(8):
        m = dict(shared)
        m.update(host_prep(inp, core))
        in_maps.append(m)
    res = run_bass_kernel_spmd(nc, in_maps, core_ids=list(range(8)))
    y_prompt = np.zeros((16, 256, DM), np.float32)
    y_sample = np.zeros((8, 2048, DM), np.float32)
    ns_ssd = np.zeros((16, 2, 2, 8, 64, 128), np.float32)
    ns_gdn = np.zeros((16, 2, 2, 4, 128, 128), np.float32)
    ns_ret = np.zeros((16, 2, 2, 4, 128, 128), np.float32)
    for core in range(8):
        r = res.results[core]
        x2 = np.asarray(r["X_2"])
        y_sample[core] = x2[0:2048]
        y_prompt[2 * core] = x2[2048:2304]
        y_prompt[2 * core + 1] = x2[2304:2560]
        for pi in range(2):
            ns_ssd[2 * core + pi] = np.asarray(r["ns_ssd"])[pi]
            ns_gdn[2 * core + pi] = np.asarray(r["ns_gdn"])[pi]
            ns_ret[2 * core + pi] = np.asarray(r["ns_ret"])[pi]
    return (y_prompt, y_sample, ns_ssd, ns_gdn, ns_ret)
```
